# Optimizing a Trainium2 kernel written in Bass

```python
import math
import jax, jax.numpy as jnp
from jax import lax
import numpy as np

D_MODEL = 4096
BATCH = 4
SEQ = 2048
DEPTH = 1

MIX_WIDTH = D_MODEL
ATTN_WIDTH = MIX_WIDTH // 2
CONV_WIDTH = MIX_WIDTH - ATTN_WIDTH
ATTN_HEAD_DIM = 64
ATTN_V_DIM = 2 * ATTN_HEAD_DIM
N_ATTN_HEADS = ATTN_WIDTH // ATTN_V_DIM
QK_WIDTH = N_ATTN_HEADS * 2 * ATTN_HEAD_DIM
CONV_K = 3
CONV_GROUPS = 16
IN_PROJ_WIDTH = 2 * QK_WIDTH + ATTN_WIDTH + 3 * CONV_WIDTH
D_FF = ((8 * D_MODEL // 3 + 255) // 256) * 256
ROPE_THETA = 10000.0
Q_BLOCK = 128
NORM_EPS = 1e-6
N_MOD = 6

kernel_name = "hybrid_diffattn_shortconv_adaln_block"


def lambda_init_fn(layer_idx):
    return 0.8 - 0.6 * math.exp(-0.3 * layer_idx)


def rms_norm(x, gain):
    xf = x.astype(jnp.float32)
    y = xf * lax.rsqrt(jnp.mean(xf * xf, axis=-1, keepdims=True) + NORM_EPS)
    return (y * gain.astype(jnp.float32)).astype(x.dtype)


def apply_rope(t, pos):
    half = t.shape[-1] // 2
    inv_freq = ROPE_THETA ** (-jnp.arange(half, dtype=jnp.float32) / half)
    ang = pos.astype(jnp.float32)[:, None] * inv_freq[None, :]
    cos = jnp.cos(ang)[None, :, None, :]
    sin = jnp.sin(ang)[None, :, None, :]
    tf = t.astype(jnp.float32)
    t1, t2 = tf[..., :half], tf[..., half:]
    out = jnp.concatenate([t1 * cos - t2 * sin, t2 * cos + t1 * sin], axis=-1)
    return out.astype(t.dtype)


def diff_attention(q1, q2, k1, k2, v, lam):
    b, h, s, dh = q1.shape
    dv = v.shape[-1]
    n_blk = s // Q_BLOCK
    scale = dh ** -0.5
    kpos = jnp.arange(s)
    neg = jnp.finfo(jnp.float32).min

    def block(args):
        qb1, qb2, start = args
        qpos = start + jnp.arange(Q_BLOCK)
        mask = kpos[None, :] <= qpos[:, None]

        def probs(qb, k):
            sc = jnp.einsum('bhqd,bhkd->bhqk', qb, k).astype(jnp.float32) * scale
            return jax.nn.softmax(jnp.where(mask, sc, neg), axis=-1)

        p = probs(qb1, k1) - lam * probs(qb2, k2)
        return jnp.einsum('bhqk,bhkd->bhqd', p.astype(v.dtype), v)

    def to_blocks(t):
        return t.reshape(b, h, n_blk, Q_BLOCK, dh).transpose(2, 0, 1, 3, 4)

    out = lax.map(block, (to_blocks(q1), to_blocks(q2), jnp.arange(n_blk) * Q_BLOCK))
    return out.transpose(1, 2, 0, 3, 4).reshape(b, h, s, dv)


def causal_depthwise_conv(u, w):
    return lax.conv_general_dilated(
        u, w[:, None, :].astype(u.dtype), window_strides=(1,),
        padding=[(CONV_K - 1, 0)], dimension_numbers=('NWC', 'WIO', 'NWC'),
        feature_group_count=u.shape[-1])


def setup_inputs(seed: int = 0) -> dict:
    key = jax.random.key(seed)
    ks = jax.random.split(key, 20)
    f32 = jnp.float32

    def nrm(k, shape, s):
        return jax.random.normal(k, shape, f32) * s

    return {
        "x": nrm(ks[0], (BATCH, SEQ, D_MODEL), 1.0),
        "c": nrm(ks[1], (BATCH, D_MODEL), 1.0),
        "w_ada": nrm(ks[2], (DEPTH, D_MODEL, N_MOD * D_MODEL), 0.5 * D_MODEL ** -0.5),
        "b_ada": nrm(ks[3], (DEPTH, N_MOD * D_MODEL), 0.02),
        "norm1_g": 1.0 + nrm(ks[4], (DEPTH, D_MODEL), 0.02),
        "w_in": nrm(ks[5], (DEPTH, D_MODEL, IN_PROJ_WIDTH), D_MODEL ** -0.5),
        "lambda_q1": nrm(ks[6], (DEPTH, ATTN_HEAD_DIM), 0.1),
        "lambda_k1": nrm(ks[7], (DEPTH, ATTN_HEAD_DIM), 0.1),
        "lambda_q2": nrm(ks[8], (DEPTH, ATTN_HEAD_DIM), 0.1),
        "lambda_k2": nrm(ks[9], (DEPTH, ATTN_HEAD_DIM), 0.1),
        "subln_g": 1.0 + nrm(ks[10], (DEPTH, ATTN_V_DIM), 0.02),
        "conv_w": nrm(ks[11], (DEPTH, CONV_K, CONV_WIDTH), CONV_K ** -0.5),
        "w_out": nrm(ks[12], (DEPTH, MIX_WIDTH, D_MODEL), MIX_WIDTH ** -0.5),
        "norm2_g": 1.0 + nrm(ks[13], (DEPTH, D_MODEL), 0.02),
        "w_gate": nrm(ks[14], (DEPTH, D_MODEL, D_FF), D_MODEL ** -0.5),
        "w_up": nrm(ks[15], (DEPTH, D_MODEL, D_FF), D_MODEL ** -0.5),
        "w_down": nrm(ks[16], (DEPTH, D_FF, D_MODEL), D_FF ** -0.5),
        "final_g": 1.0 + nrm(ks[17], (D_MODEL,), 0.02),
    }


def reference(x, c, w_ada, b_ada, norm1_g, w_in, lambda_q1, lambda_k1, lambda_q2,
              lambda_k2, subln_g, conv_w, w_out, norm2_g, w_gate, w_up, w_down, final_g):
    b, s, _ = x.shape
    pos = jnp.arange(s)
    c_act = jax.nn.silu(c)
    splits = np.cumsum([QK_WIDTH, QK_WIDTH, ATTN_WIDTH, CONV_WIDTH, CONV_WIDTH]).tolist()

    for l in range(DEPTH):
        lam_init = lambda_init_fn(l)
        mod = (c_act @ w_ada[l] + b_ada[l])[:, None, :]
        sh1, sc1, g1, sh2, sc2, g2 = jnp.split(mod, N_MOD, axis=-1)

        h = rms_norm(x, norm1_g[l]) * (1.0 + sc1) + sh1
        proj = h @ w_in[l]
        q, k, v, bg, cg, xg = jnp.split(proj, splits, axis=-1)

        q = q.reshape(b, s, N_ATTN_HEADS, 2, ATTN_HEAD_DIM)
        k = k.reshape(b, s, N_ATTN_HEADS, 2, ATTN_HEAD_DIM)
        v = v.reshape(b, s, N_ATTN_HEADS, ATTN_V_DIM)
        q1 = apply_rope(q[..., 0, :], pos).transpose(0, 2, 1, 3)
        q2 = apply_rope(q[..., 1, :], pos).transpose(0, 2, 1, 3)
        k1 = apply_rope(k[..., 0, :], pos).transpose(0, 2, 1, 3)
        k2 = apply_rope(k[..., 1, :], pos).transpose(0, 2, 1, 3)
        lam = (jnp.exp(jnp.sum(lambda_q1[l].astype(jnp.float32) * lambda_k1[l].astype(jnp.float32)))
               - jnp.exp(jnp.sum(lambda_q2[l].astype(jnp.float32) * lambda_k2[l].astype(jnp.float32)))
               + lam_init)
        attn = diff_attention(q1, q2, k1, k2, v.transpose(0, 2, 1, 3), lam)
        attn = rms_norm(attn, subln_g[l]) * (1.0 - lam_init)
        attn = attn.transpose(0, 2, 1, 3).reshape(b, s, ATTN_WIDTH)

        conv = bg * causal_depthwise_conv(cg * xg, conv_w[l])

        mix = jnp.concatenate([attn, conv], axis=-1) @ w_out[l]
        x = x + g1 * mix

        h = rms_norm(x, norm2_g[l]) * (1.0 + sc2) + sh2
        ffn = (jax.nn.silu(h @ w_gate[l]) * (h @ w_up[l])) @ w_down[l]
        x = x + g2 * ffn

    return rms_norm(x, final_g)
```

```python
from contextlib import ExitStack

import sys
import ml_dtypes
import numpy as np

import concourse.bass as bass
import concourse.mybir as mybir
from concourse.bass_utils import run_bass_kernel_spmd

F32 = mybir.dt.float32
BF16 = mybir.dt.bfloat16
AF = mybir.ActivationFunctionType
ALU = mybir.AluOpType
AX = mybir.AxisListType

NEG = -30000.0
DEBUG_W = False
STOP = 99


class _StopBuild(Exception):
    pass
LINEMAP = {}
NORM_EPS = 1e-6
ROPE_THETA = 10000.0
LAM_INIT = 0.8 - 0.6 * 1.0


class Cfg:
    def __init__(self, D=4096, S=2048, H=16, FF=11008):
        self.D, self.S, self.H, self.FF = D, S, H, FF
        self.KC = D // 128
        self.TOWN = S // 2
        self.TALL = S
        self.NB = self.TOWN // 128
        self.AW = H * 128
        self.QKW = H * 128
        self.CW = D - self.AW
        self.CC = self.CW // 128
        self.INW = 2 * self.QKW + self.AW + 3 * self.CW
        self.FC = FF // 128
        self.TG = min(512, self.TOWN)
        self.PASS = min(512, self.TOWN)
        self.WCAP = 8192
        self.NWS = 3


class _Rec:
    def __init__(self):
        self.call = None

    def __getattr__(self, name):
        def f(*a, **k):
            assert self.call is None
            self.call = (name, a, k)
            return self
        return f


def _record(fn):
    if fn is None:
        return None
    r = _Rec()
    fn(r)
    assert r.call is not None
    return r.call


class Sched:
    ENG = ("pe", "act", "dve", "pool", "sp")

    def __init__(self):
        self.ops = {e: [] for e in self.ENG}
        self.cnt = {e: 0 for e in self.ENG}
        self.seen = {e: {} for e in self.ENG}
        self.dcnt = {}

    @staticmethod
    def _flat(deps, outl):
        if deps is None:
            return outl
        if isinstance(deps, tuple) and len(deps) == 2 and isinstance(deps[0], str) and isinstance(deps[1], int):
            outl.append(deps)
            return outl
        for d in deps:
            Sched._flat(d, outl)
        return outl

    def _waits(self, eng, deps):
        w = []
        for d in self._flat(deps, []):
            key, val = d
            if self.seen[eng].get(key, 0) >= val:
                continue
            self.seen[eng][key] = val
            w.append((key, val))
        return w

    def op(self, eng, fn, deps=(), signal=True):
        w = self._waits(eng, deps)
        tok = None
        if signal:
            self.cnt[eng] += 1
            tok = (eng, self.cnt[eng])
        self.ops[eng].append((w, _record(fn), eng if signal else None, 1, sys._getframe(1).f_lineno))
        return tok

    def dma(self, eng, fn, sem, deps=()):
        w = self._waits(eng, deps)
        self.dcnt[sem] = self.dcnt.get(sem, 0) + 16
        self.ops[eng].append((w, _record(fn), sem, 16, sys._getframe(1).f_lineno))
        return (sem, self.dcnt[sem])

    def last(self, eng):
        return (eng, self.cnt[eng]) if self.cnt[eng] else None

    def barrier(self, engines=("pe", "act", "dve", "sp")):
        toks = [self.last(e) for e in engines]
        toks += [(s, v) for s, v in self.dcnt.items()]
        for e in engines:
            w = self._waits(e, toks)
            if w:
                self.ops[e].append((w, None, None, 0, 0))

    def emit(self, block, sems):
        def run(engobj, name):
            for (w, fn, inc, amt, srcl) in self.ops[name]:
                for key, val in w:
                    engobj.wait_ge(sems[key], val)
                if fn is None:
                    continue
                name_, a_, k_ = fn
                ins = getattr(engobj, name_)(*a_, **k_)
                LINEMAP[ins.ins.name] = srcl
                if inc is not None:
                    ins.then_inc(sems[inc], amt)

        @block.tensor
        def _(e):
            run(e, "pe")

        @block.scalar
        def _(e):
            run(e, "act")

        @block.vector
        def _(e):
            run(e, "dve")

        @block.gpsimd
        def _(e):
            run(e, "pool")

        @block.sync
        def _(e):
            run(e, "sp")


class Ring:
    def __init__(self, items):
        self.items = list(items)
        self.free = [None] * len(self.items)
        self.i = 0

    def get(self):
        i = self.i
        self.i = (i + 1) % len(self.items)
        return i, self.items[i], self.free[i]

    def release(self, i, tok):
        self.free[i] = tok


def build(cfg):
    nc = bass.Bass("TRN2", target_bir_lowering=False)
    D, KC, TOWN, TALL, NB, H = cfg.D, cfg.KC, cfg.TOWN, cfg.TALL, cfg.NB, cfg.H
    QKW, AW, CW, CC, INW, FF, FC, TG, PASS = cfg.QKW, cfg.AW, cfg.CW, cfg.CC, cfg.INW, cfg.FF, cfg.FC, cfg.TG, cfg.PASS
    NG_OWN = TOWN // TG
    NPASS = TOWN // PASS
    PB = PASS // 128

    def din(name, shape, dt=F32):
        return nc.dram_tensor(name, list(shape), dt, kind="ExternalInput").ap()

    xin = din("xin", [TALL, D])
    cT = din("cT", [128, KC])
    w_ada = din("w_ada", [D, 6 * D])
    badaT = din("badaT", [128, 6 * KC])
    n1gT = din("n1gT", [128, KC])
    n2gT = din("n2gT", [128, KC])
    fgv = din("fgv", [D])
    w_in = din("w_in", [D, INW])
    w_out = din("w_out", [D, D])
    w_gate = din("w_gate", [D, FF])
    w_up = din("w_up", [D, FF])
    w_down = din("w_down", [FF, D])
    lamv = din("lamv", [4 * 64])
    sublng = din("sublng", [128])
    convwT = din("convwT", [128, CC * 3])
    cosT = din("cosT", [128, TALL])
    sinsT = din("sinsT", [128, TALL])
    identf_d = din("identf", [128, 128])
    identb_d = din("identb", [128, 128], BF16)
    permb_d = din("permb", [128, 128], BF16)
    trimask_d = din("trimask", [128, 256], BF16)
    flags_d = din("flags", [128, 2])
    out = nc.dram_tensor("out", [TOWN, D], F32, kind="ExternalOutput").ap()

    q1_scr = nc.dram_tensor("q1_scr", [H, 64, TOWN], BF16).ap()
    q2_scr = nc.dram_tensor("q2_scr", [H, 64, TOWN], BF16).ap()
    k_scr = nc.dram_tensor("k_scr", [H, 128, TALL], BF16).ap()
    v_scr = nc.dram_tensor("v_scr", [TALL, AW], BF16).ap()
    y_scr = nc.dram_tensor("y_scr", [CC, 128, TOWN], BF16).ap()
    x1_scr = nc.dram_tensor("x1_scr", [TOWN, D], F32).ap()
    mod_scr = nc.dram_tensor("mod_scr", [6 * D], F32).ap()

    S = Sched()
    es = ExitStack()
    sems = {}

    def mksem(name):
        sems[name] = es.enter_context(nc.semaphore(name))
        return name

    for e in Sched.ENG:
        mksem(e)

    ARENA = 72 * 1024
    arena = es.enter_context(nc.sbuf_tensor("arena", [128, ARENA], BF16))
    wpool = es.enter_context(nc.sbuf_tensor("wpool", [128, cfg.NWS, cfg.WCAP], BF16))
    consts = es.enter_context(nc.sbuf_tensor("consts", [128, 2048], F32))
    constb = es.enter_context(nc.sbuf_tensor("constb", [128, 512], BF16))
    psum = es.enter_context(nc.psum_tensor("psum", [128, 8, 512], F32))

    class Carver:
        def __init__(self):
            self.off = 0

        def take(self, n_bf16):
            o = self.off
            self.off += (n_bf16 + 15) // 16 * 16
            assert self.off <= ARENA, (self.off, ARENA)
            return o

        def bf(self, shape):
            n = int(np.prod(shape))
            o = self.take(n)
            ap = arena[:, o:o + n]
            return self._shape(ap, shape)

        def f32(self, shape):
            n = int(np.prod(shape))
            o = self.take(2 * n)
            ap = arena[:, o:o + 2 * n].bitcast(F32)
            return self._shape(ap, shape)

        @staticmethod
        def _shape(ap, shape):
            if len(shape) == 1:
                return ap
            if len(shape) == 2:
                return ap.rearrange("p (a b) -> p a b", a=shape[0])
            if len(shape) == 3:
                return ap.rearrange("p (a b c) -> p a b c", a=shape[0], b=shape[1])
            raise ValueError(shape)

    cpos = [0]

    def ctake(n):
        o = cpos[0]
        cpos[0] += n
        assert cpos[0] <= 2048
        return consts[:, o:o + n]

    identf = ctake(128)
    modT = ctake(6 * KC)
    badaT_s = ctake(6 * KC)
    n1g_s = ctake(KC)
    n2g_s = ctake(KC)
    gm1 = ctake(KC)
    gm2 = ctake(KC)
    cact = ctake(KC)
    convw_s = ctake(CC * 3)
    flags_s = ctake(2)
    lam_s = ctake(256)
    lamw = ctake(16)
    gsub = ctake(128)
    epsb = ctake(1)
    small = ctake(64 + 32 * 8)
    bpos = [0]

    def btake(n):
        o = bpos[0]
        bpos[0] += n
        assert bpos[0] <= 512
        return constb[:, o:o + n]

    identb = btake(128)
    permb = btake(128)
    trimask2 = btake(256)

    wring = Ring(range(cfg.NWS))
    for i in range(cfg.NWS):
        mksem(f"w{i}")

    def wload(src_ap, kc, ncols):
        assert kc * ncols <= cfg.WCAP
        i, slot, free = wring.get()
        if DEBUG_W: print('wload', i, free, src_ap.tensor.name, S.last('pe'))
        dst = wpool[:, slot, 0:kc * ncols].rearrange("p (k n) -> p k n", k=kc)
        tok = S.dma("pool", lambda e, dst=dst, src=src_ap: e.dma_start(out=dst, in_=src), f"w{i}", deps=[free])
        return dst, tok, i

    def wsrc(w, r0, kc, c0, ncols):
        return w[r0:r0 + kc * 128, c0:c0 + ncols].rearrange("(k p) n -> p k n", p=128)

    dsem_i = [0]

    def newsem(prefix):
        dsem_i[0] += 1
        return mksem(f"{prefix}{dsem_i[0]}")

    out_toks = []
    try:
        s_c = newsem("c")

        def ld(dst, src, eng="sp", sem=None, deps=()):
            return S.dma(eng, lambda e, dst=dst, src=src: e.dma_start(out=dst, in_=src), sem or s_c, deps=deps)

        ld(identf, identf_d)
        ld(badaT_s, badaT)
        ld(n1g_s, n1gT)
        ld(n2g_s, n2gT)
        ld(cact, cT)
        ld(convw_s, convwT)
        ld(flags_s, flags_d)
        ld(lam_s, lamv.partition_broadcast(128))
        ld(gsub, sublng.partition_broadcast(128))
        ld(identb, identb_d)
        ld(permb, permb_d)
        t_c = ld(trimask2, trimask_d)

        t = S.op("dve", lambda e: e.memset(epsb, NORM_EPS), deps=[t_c])
        t = S.op("dve", lambda e: e.tensor_tensor(out=lam_s[:, 0:64], in0=lam_s[:, 0:64], in1=lam_s[:, 64:128], op=ALU.mult), deps=[t])
        t = S.op("dve", lambda e: e.tensor_tensor(out=lam_s[:, 128:192], in0=lam_s[:, 128:192], in1=lam_s[:, 192:256], op=ALU.mult), deps=[t])
        t = S.op("dve", lambda e: e.tensor_reduce(out=lamw[:, 0:1], in_=lam_s[:, 0:64], axis=AX.X, op=ALU.add), deps=[t])
        t = S.op("dve", lambda e: e.tensor_reduce(out=lamw[:, 1:2], in_=lam_s[:, 128:192], axis=AX.X, op=ALU.add), deps=[t])
        t = S.op("act", lambda e: e.activation(out=lamw[:, 2:4], in_=lamw[:, 0:2], func=AF.Exp), deps=[t])
        t = S.op("dve", lambda e: e.tensor_tensor(out=lamw[:, 4:5], in0=lamw[:, 3:4], in1=lamw[:, 2:3], op=ALU.subtract), deps=[t])
        t = S.op("dve", lambda e: e.tensor_scalar(out=lamw[:, 4:5], in0=lamw[:, 4:5], scalar1=-LAM_INIT, scalar2=None, op0=ALU.add), deps=[t])
        t = S.op("dve", lambda e: e.tensor_scalar(out=gsub, in0=gsub, scalar1=1.0 - LAM_INIT, scalar2=None, op0=ALU.mult), deps=[t])
        t_consts = t

        if STOP == 0:
            S.barrier()
            raise _StopBuild()
        car = Carver()
        car.off = ARENA - (KC * 128 + 2 * 2 * 512)
        ADA_LIMIT = car.off
        cb = car.bf([KC, 128])
        modtmp = car.f32([2, 512])
        car.off = 0
        off_stage = car.off
        t_cb = S.op("act", lambda e: e.activation(out=cb, in_=cact.unsqueeze(2).to_broadcast([128, KC, 128]), func=AF.Silu),
                    deps=[t_c])
        pring_mod = Ring([7])
        AK = getattr(cfg, "AK", min(KC, 16))
        ANC = 512 if D >= 512 else D
        n_ktiles = KC // AK
        mtmp_ring = Ring([0, 1])

        ada_state = {}

        def ada_tile(j0):
            if not ada_state:
                bi, bank, bfree = pring_mod.get()
                ada_state.update(bi=bi, bank=bank, bfree=bfree, kt=0)
            st_ = ada_state
            kt = st_["kt"]
            ps = psum[:, st_["bank"], 0:ANC]
            wt, wtok, wi = wload(wsrc(w_ada, kt * AK * 128, AK, j0 * ANC, ANC), AK, ANC)
            last = None
            for k in range(AK):
                kk = kt * AK + k
                last = S.op("pe", lambda e: e.matmul(ps, lhsT=cb[:, kk, :], rhs=wt[:, k, :], start=(kk == 0), stop=(kk == KC - 1)),
                            deps=[wtok, t_cb, st_["bfree"]] if k == 0 else (), signal=(k == AK - 1))
            wring.release(wi, last)
            st_["kt"] = kt + 1
            if st_["kt"] < n_ktiles:
                return None
            nch = ANC // 128
            mi, ms, mfree = mtmp_ring.get()
            mt = modtmp[:, ms, 0:ANC].rearrange("p (a b) -> p a b", a=nch)
            t1 = S.op("dve", lambda e: e.tensor_tensor(out=mt, in0=ps.rearrange("p (a b) -> p a b", a=nch),
                                                       in1=identf.unsqueeze(1).to_broadcast([128, nch, 128]), op=ALU.mult),
                      deps=[last, mfree, t_c])
            pring_mod.release(st_["bi"], t1)
            c0 = j0 * nch
            t2 = S.op("dve", lambda e: e.tensor_reduce(out=modT[:, c0:c0 + nch], in_=mt, axis=AX.X, op=ALU.add), deps=[t1])
            t3 = S.op("dve", lambda e: e.tensor_tensor(out=modT[:, c0:c0 + nch], in0=modT[:, c0:c0 + nch],
                                                       in1=badaT_s[:, c0:c0 + nch], op=ALU.add), deps=[t2])
            mtmp_ring.release(mi, t3)
            ada_state.clear()
            return t3

        def ada_group(j0):
            t_ = None
            while t_ is None:
                t_ = ada_tile(j0)
            return t_

        NGRP = 6 * D // ANC
        GPM = D // ANC
        t = None
        for j in range(2 * GPM):
            t = ada_group(j)
        t = S.op("dve", lambda e: e.scalar_tensor_tensor(out=gm1, in0=modT[:, KC:2 * KC], scalar=1.0, in1=n1g_s,
                                                         op0=ALU.add, op1=ALU.mult), deps=[t, t_c])
        t_gm1 = t
        ada_next = [2 * GPM]
        ada_done = [None]

        def ada_step(limit):
            if ada_next[0] < min(limit, NGRP):
                t_ = ada_tile(ada_next[0])
                if t_ is not None:
                    ada_done[0] = t_
                    ada_next[0] += 1

        def ada_more(n=1):
            for _ in range(n):
                if ada_next[0] < NGRP:
                    ada_done[0] = ada_group(ada_next[0])
                    ada_next[0] += 1

        if STOP == 1:
            S.barrier()
            raise _StopBuild()
        def norm_transpose(car0, src_rows, nblk, dstT, gm, shift, t_dst_free, t_src=None, extra_deps=(), nslots=3):
            c = Carver()
            c.off = car0
            xs = c.f32([nslots, D])
            junk = c.bf([D])
            xring = Ring(range(nslots))
            sems_x = [newsem("x") for _ in range(nslots)]
            pr = Ring([2, 3, 4, 5])
            stat = small
            def stats(b):
                xi, xslot, xfree = xring.get()
                xb = xs[:, xslot, :]
                tl = S.dma("sp", lambda e: e.dma_start(out=xb, in_=src_rows[b * 128:(b + 1) * 128, :]),
                           sems_x[xi], deps=[xfree, t_src, t_dst_free] + list(extra_deps))
                c0 = (b % 8) * 4
                t1 = S.op("act", lambda e: e.activation(out=junk, in_=xb, func=AF.Square, accum_out=stat[:, c0:c0 + 1]), deps=[tl])
                t2 = S.op("act", lambda e: e.activation(out=stat[:, c0 + 1:c0 + 2], in_=stat[:, c0:c0 + 1], func=AF.Sqrt,
                                                        scale=1.0 / D, bias=epsb), deps=[t1, t_consts])
                t3 = S.op("dve", lambda e: e.reciprocal(out=stat[:, c0 + 2:c0 + 3], in_=stat[:, c0 + 1:c0 + 2]), deps=[t2])
                t4 = S.op("act", lambda e: e.activation(out=xb, in_=xb, func=AF.Copy, scale=stat[:, c0 + 2:c0 + 3]), deps=[t3])
                return dict(xi=xi, xb=xb, tl=tl, t4=t4)

            def mm_evac(b, sb):
                xb = sb["xb"]
                lastev = None
                for k4 in range(0, KC, 4):
                    bi, bank, bfree = pr.get()
                    tp = None
                    for q in range(4):
                        k = k4 + q
                        tp = S.op("pe", lambda e: e.transpose(out=psum[:, bank, q * 128:(q + 1) * 128], in_=xb[:, k * 128:(k + 1) * 128], identity=identf),
                                  deps=[sb["t4"], sb["tl"], bfree, t_c] if q == 0 else (), signal=(q == 3))
                    eng = "act" if ((k4 // 4) % 8) == 0 else "dve"
                    for q in range(4):
                        k = k4 + q
                        dst = dstT[:, k, b * 128:(b + 1) * 128]
                        src = psum[:, bank, q * 128:(q + 1) * 128]
                        if eng == "act":
                            lastev = S.op("act", lambda e: e.activation(out=dst, in_=src, func=AF.Identity,
                                                                        scale=gm[:, k:k + 1], bias=shift[:, k:k + 1]), deps=[tp, t_gm1])
                        else:
                            lastev = S.op("dve", lambda e: e.tensor_scalar(out=dst, in0=src, scalar1=gm[:, k:k + 1],
                                                                           scalar2=shift[:, k:k + 1], op0=ALU.mult, op1=ALU.add), deps=[tp, t_gm1])
                    pr.release(bi, lastev)
                xring.release(sb["xi"], S.last("pe"))

            nxt = stats(0)
            for b in range(nblk):
                cur = nxt
                if b + 1 < nblk:
                    nxt = stats(b + 1)
                mm_evac(b, cur)
            return (S.last("act"), S.last("dve"))

        car.off = off_stage
        hT = car.bf([KC, TOWN])
        halo_h = car.bf([KC, 2])
        off3 = car.off

        QOFF, KOFF, VOFF = 0, QKW, 2 * QKW
        BOFF, COFF, XOFF = 2 * QKW + AW, 2 * QKW + AW + CW, 2 * QKW + AW + 2 * CW
        WNC = min(256, cfg.WCAP // KC)

        def in_proj(half):
            tok0 = TOWN if half == 0 else 0
            ntile = [0]

            def tick():
                ntile[0] += 1
                if ntile[0] % 2 == 0:
                    ada_step(4 * GPM)

            c = Carver()
            c.off = off3
            cos_s = c.f32([TOWN])
            sin_s = c.f32([TOWN])
            Tb = c.bf([2, TG])
            tmpf = c.f32([2, TG])
            qst = c.bf([2, TG])
            vst = c.bf([2, WNC])
            csb = c.f32([WNC // 128, TOWN + 2])
            usb = c.f32([WNC // 128, TOWN + 2])
            ycv = c.f32([WNC // 128, TOWN])
            yst = c.bf([WNC // 128, TOWN])
            s_t = newsem("t")
            t_tab = ld(cos_s, cosT[:, tok0:tok0 + TOWN], sem=s_t, deps=[S.last("dve"), S.last("act")])
            t_tab = ld(sin_s, sinsT[:, tok0:tok0 + TOWN], sem=s_t)
            pr = Ring([0, 1, 2, 3])
            pr2 = Ring([4, 5])
            tring = Ring([0, 1])
            qring = Ring([0, 1])
            vring = Ring([0, 1])
            sem_q = [newsem("q"), newsem("q")]
            sem_v = [newsem("v"), newsem("v")]
            groups = [(g * TG, TG) for g in range(TOWN // TG)]
            store_toks = []

            def rope_chunk(wt, wtok, cc, head, is_q):
                for (t0, n) in groups:
                    bi, bank, bfree = pr.get()
                    ps = psum[:, bank, 0:n]
                    last = None
                    for k in range(KC):
                        last = S.op("pe", lambda e, ps=ps, k=k, t0=t0, n=n: e.matmul(ps, lhsT=wt[:, k, cc * 128:(cc + 1) * 128],
                                                                                     rhs=hT[:, k, t0:t0 + n], start=(k == 0), stop=(k == KC - 1)),
                                    deps=[wtok, bfree, t_h] if k == 0 else (), signal=(k == KC - 1))
                    ti, tslot, tfree = tring.get()
                    T = Tb[:, tslot, 0:n]
                    t1 = S.op("act", lambda e: e.activation(out=T, in_=ps, func=AF.Copy), deps=[last, tfree])
                    pr.release(bi, t1)
                    if pending_rope:
                        pending_rope.pop(0)()
                    pending_rope.append(lambda T=T, t1=t1, ti=ti, tslot=tslot, tfree=tfree, t0=t0, n=n, head=head, is_q=is_q:
                                        rope_finish(T, t1, ti, tslot, tfree, t0, n, head, is_q))

            def rope_finish(T, t1, ti, tslot, tfree, t0, n, head, is_q):
                b2i, bank2, b2free = pr2.get()
                ps2 = psum[:, bank2, 0:n]
                t2 = S.op("pe", lambda e: e.matmul(ps2, lhsT=permb, rhs=T, start=True, stop=True), deps=[t1, b2free, t_c])
                tm = tmpf[:, tslot, 0:n]
                t3 = S.op("dve", lambda e: e.tensor_tensor(out=tm, in0=T, in1=cos_s[:, t0:t0 + n], op=ALU.mult), deps=[t1, t_tab, tfree])
                qi, qslot, qfree = qring.get()
                qo = qst[:, qslot, 0:n]
                t4 = S.op("dve", lambda e: e.tensor_tensor(out=qo, in0=ps2, in1=sin_s[:, t0:t0 + n], op=ALU.mult), deps=[t2, qfree])
                pr2.release(b2i, t4)
                t5 = S.op("dve", lambda e: e.tensor_tensor(out=qo, in0=qo, in1=tm, op=ALU.add), deps=[t4, t3])
                tring.release(ti, (t2, t5))
                if is_q:
                    d1 = S.dma("sp", lambda e: e.dma_start(out=q1_scr[head, :, t0:t0 + n], in_=qo[0:64, :]), sem_q[qi], deps=[t5])
                    d2 = S.dma("sp", lambda e: e.dma_start(out=q2_scr[head, :, t0:t0 + n], in_=qo[64:128, :]), sem_q[qi], deps=[t5])
                    qring.release(qi, d2)
                    store_toks.append(d2)
                else:
                    d1 = S.dma("sp", lambda e: e.dma_start(out=k_scr[head, :, tok0 + t0:tok0 + t0 + n], in_=qo), sem_q[qi], deps=[t5])
                    qring.release(qi, d1)
                    store_toks.append(d1)

            pending_rope = []
            for (off, is_q) in ((QOFF, True), (KOFF, False)):
                if is_q and half == 0:
                    continue
                for c0 in range(0, QKW, WNC):
                    wt, wtok, wi = wload(wsrc(w_in, 0, KC, off + c0, WNC), KC, WNC)
                    for cc in range(WNC // 128):
                        rope_chunk(wt, wtok, cc, (c0 // 128) + cc, is_q)
                    wring.release(wi, S.last("pe"))
                    tick()
            while pending_rope:
                pending_rope.pop(0)()
            for c0 in range(0, AW, WNC):
                wt, wtok, wi = wload(wsrc(w_in, 0, KC, VOFF + c0, WNC), KC, WNC)
                for tb in range(NB):
                    bi, bank, bfree = pr.get()
                    ps = psum[:, bank, 0:WNC]
                    last = None
                    for k in range(KC):
                        last = S.op("pe", lambda e, ps=ps, k=k, tb=tb: e.matmul(ps, lhsT=hT[:, k, tb * 128:(tb + 1) * 128], rhs=wt[:, k, :],
                                                                                start=(k == 0), stop=(k == KC - 1)),
                                    deps=[wtok, bfree, t_h] if k == 0 else (), signal=(k == KC - 1))
                    vi, vslot, vfree = vring.get()
                    vo = vst[:, vslot, :]
                    eng = "act" if tb % 2 == 0 else "dve"
                    if eng == "act":
                        t1 = S.op("act", lambda e, vo=vo, ps=ps: e.activation(out=vo, in_=ps, func=AF.Copy), deps=[last, vfree])
                    else:
                        t1 = S.op("dve", lambda e, vo=vo, ps=ps: e.tensor_copy(out=vo, in_=ps), deps=[last, vfree])
                    pr.release(bi, t1)
                    r0 = tok0 + tb * 128
                    d1 = S.dma("sp", lambda e, vo=vo, r0=r0, c0=c0: e.dma_start(out=v_scr[r0:r0 + 128, c0:c0 + WNC], in_=vo), sem_v[vi], deps=[t1])
                    vring.release(vi, d1)
                    store_toks.append(d1)
                wring.release(wi, S.last("pe"))
                tick()
            if half == 0:
                return store_toks
            nch = WNC // 128
            sem_y = newsem("y")
            y_free = None
            cs_free = None
            us_free = None
            for c0 in range(0, CW, WNC):
                wt, wtok, wi = wload(wsrc(w_in, 0, KC, COFF + c0, WNC), KC, WNC)
                tC = []
                for cc in range(nch):
                    for gi, (t0, n) in enumerate(groups + [(-2, 2)]):
                        bi, bank, bfree = pr.get()
                        ps = psum[:, bank, 0:n]
                        last = None
                        for k in range(KC):
                            rhs = hT[:, k, t0:t0 + n] if t0 >= 0 else halo_h[:, k, :]
                            last = S.op("pe", lambda e, ps=ps, k=k, rhs=rhs, cc=cc: e.matmul(ps, lhsT=wt[:, k, cc * 128:(cc + 1) * 128], rhs=rhs,
                                                                                            start=(k == 0), stop=(k == KC - 1)),
                                        deps=[wtok, bfree, t_h] if k == 0 else (), signal=(k == KC - 1))
                        dst = csb[:, cc, 2 + t0:2 + t0 + n]
                        t1 = S.op("act", lambda e, dst=dst, ps=ps: e.activation(out=dst, in_=ps, func=AF.Copy), deps=[last, cs_free])
                        pr.release(bi, t1)
                        tC.append(t1)
                wring.release(wi, S.last("pe"))
                tick()
                wt, wtok, wi = wload(wsrc(w_in, 0, KC, XOFF + c0, WNC), KC, WNC)
                tU = []
                for cc in range(nch):
                    for gi, (t0, n) in enumerate(groups + [(-2, 2)]):
                        bi, bank, bfree = pr.get()
                        ps = psum[:, bank, 0:n]
                        last = None
                        for k in range(KC):
                            rhs = hT[:, k, t0:t0 + n] if t0 >= 0 else halo_h[:, k, :]
                            last = S.op("pe", lambda e, ps=ps, k=k, rhs=rhs, cc=cc: e.matmul(ps, lhsT=wt[:, k, cc * 128:(cc + 1) * 128], rhs=rhs,
                                                                                            start=(k == 0), stop=(k == KC - 1)),
                                        deps=[wtok, bfree, t_h] if k == 0 else (), signal=(k == KC - 1))
                        dst = usb[:, cc, 2 + t0:2 + t0 + n]
                        src = csb[:, cc, 2 + t0:2 + t0 + n]
                        t1 = S.op("dve", lambda e, dst=dst, ps=ps, src=src: e.tensor_tensor(out=dst, in0=ps, in1=src, op=ALU.mult),
                                  deps=[last, us_free] + tC)
                        pr.release(bi, t1)
                        if t0 < 0:
                            t1 = S.op("dve", lambda e, dst=dst: e.tensor_scalar(out=dst, in0=dst, scalar1=flags_s[:, 1:2], scalar2=None, op0=ALU.mult),
                                      deps=[t1, t_c])
                        tU.append(t1)
                wring.release(wi, S.last("pe"))
                tick()
                cs_free = tU[-1]
                tY = []
                for cc in range(nch):
                    ch = c0 // 128 + cc
                    w0 = convw_s[:, ch * 3 + 0:ch * 3 + 1]
                    w1 = convw_s[:, ch * 3 + 1:ch * 3 + 2]
                    w2 = convw_s[:, ch * 3 + 2:ch * 3 + 3]
                    yv = ycv[:, cc, :]
                    t1 = S.op("dve", lambda e, yv=yv, cc=cc, w2=w2: e.tensor_scalar(out=yv, in0=usb[:, cc, 2:2 + TOWN], scalar1=w2, scalar2=None, op0=ALU.mult),
                              deps=tU + [y_free, t_c])
                    t2 = S.op("dve", lambda e, yv=yv, cc=cc, w1=w1: e.scalar_tensor_tensor(out=yv, in0=usb[:, cc, 1:1 + TOWN], scalar=w1, in1=yv,
                                                                                          op0=ALU.mult, op1=ALU.add), deps=[t1])
                    t3 = S.op("dve", lambda e, yv=yv, cc=cc, w0=w0: e.scalar_tensor_tensor(out=yv, in0=usb[:, cc, 0:TOWN], scalar=w0, in1=yv,
                                                                                          op0=ALU.mult, op1=ALU.add), deps=[t2])
                    tY.append(t3)
                us_free = tY[-1]
                wt, wtok, wi = wload(wsrc(w_in, 0, KC, BOFF + c0, WNC), KC, WNC)
                tS = []
                for cc in range(nch):
                    for (t0, n) in groups:
                        bi, bank, bfree = pr.get()
                        ps = psum[:, bank, 0:n]
                        last = None
                        for k in range(KC):
                            last = S.op("pe", lambda e, ps=ps, k=k, t0=t0, n=n, cc=cc: e.matmul(ps, lhsT=wt[:, k, cc * 128:(cc + 1) * 128],
                                                                                               rhs=hT[:, k, t0:t0 + n], start=(k == 0), stop=(k == KC - 1)),
                                        deps=[wtok, bfree, t_h] if k == 0 else (), signal=(k == KC - 1))
                        dst = yst[:, cc, t0:t0 + n]
                        src = ycv[:, cc, t0:t0 + n]
                        t1 = S.op("dve", lambda e, dst=dst, ps=ps, src=src: e.tensor_tensor(out=dst, in0=ps, in1=src, op=ALU.mult),
                                  deps=[last, y_free] + tY)
                        pr.release(bi, t1)
                        tS.append(t1)
                wring.release(wi, S.last("pe"))
                tick()
                d = None
                for cc in range(nch):
                    ch = c0 // 128 + cc
                    d = S.dma("sp", lambda e, cc=cc, ch=ch: e.dma_start(out=y_scr[ch, :, :], in_=yst[:, cc, :]), sem_y, deps=tS)
                y_free = d
                store_toks.append(d)
            return store_toks

        all_stores = []
        for half in (0, 1):
            src = xin[TOWN:TALL, :] if half == 0 else xin[0:TOWN, :]
            t_h = norm_transpose(car.off if False else off3, src, NB, hT, gm1, modT[:, 0:KC], t_dst_free=(S.last("pe")))
            if half == 0:
                t_halo = S.op("dve", lambda e: e.tensor_copy(out=halo_h, in_=hT[:, :, TOWN - 2:TOWN]), deps=[t_h])
                t_h = (t_h, t_halo)
            if STOP == 20 + half:
                S.barrier()
                raise _StopBuild()
            all_stores += in_proj(half)
            if STOP == 30 + half:
                S.barrier()
                raise _StopBuild()
            S.barrier()
        while ada_next[0] < 3 * GPM:
            ada_more(1)
        s_m = newsem("m")
        mod_scr_T = mod_scr.rearrange("(j p) -> p j", p=128)
        t_modscr1 = S.dma("sp", lambda e: e.dma_start(out=mod_scr_T[:, 2 * KC:3 * KC], in_=modT[:, 2 * KC:3 * KC]), s_m, deps=[ada_done[0]])
        S.barrier()

        if STOP == 3:
            S.barrier()
            raise _StopBuild()
        car.off = 0
        catT = car.bf([KC, TOWN])
        off4 = car.off
        qq = car.bf([2, 2, TOWN])
        kh = car.bf([2, TALL])
        vraw = car.bf([2, TALL // 128, 128])
        vh = car.bf([2, TALL // 128, 129])
        ptl = car.bf([3, 512])
        o1n = car.f32([2, 128])
        osb = car.f32([2, 128])
        onb = car.bf([2, 128])
        junk4 = car.f32([128])
        s_y2 = newsem("yl")

        def load_y_chunks():
            for ch in range(CC):
                ld(catT[:, H + ch, :], y_scr[ch, :, :], sem=s_y2, deps=all_stores[-1:])
        t = S.op("dve", lambda e: e.memset(qq[64:128, :, 0, :], 0.0))
        t = S.op("dve", lambda e: e.memset(qq[0:64, :, 1, :], 0.0))
        t_init = S.op("dve", lambda e: e.memset(vh[:, :, :, 128:129], 1.0))
        sem_h = [newsem("h"), newsem("h")]
        hring = Ring([0, 1])
        sring = Ring([0, 1, 2, 3])
        oring = Ring([4, 5])
        pring = Ring([0, 1, 2])
        ering = Ring([0, 1])
        NOB = NB
        st = small[:, 64:64 + 32 * 8]
        eidx = [0]
        tring4 = Ring([6])
        tasks = []
        for h in range(H):
            for i in range(NB):
                kb_oth = [(NB + j, True, False) for j in range(NOB)]
                kb_own = [(j, False, j == i) for j in range(i + 1)]
                grps = [kb_oth[a:a + 2] for a in range(0, len(kb_oth), 2)] + [kb_own[a:a + 2] for a in range(0, len(kb_own), 2)]
                for gi_, grp in enumerate(grps):
                    tasks.append(dict(h=h, i=i, grp=grp, first=(gi_ == 0), last=(gi_ == len(grps) - 1)))
        head_state = {}
        qb_state = {}
        pending_T = []

        def head_prologue(h):
            hi, hs, hfree = hring.get()
            tq = S.dma("sp", lambda e: e.dma_start(out=qq[0:64, hs, 0, :], in_=q1_scr[h, :, :]), sem_h[hi], deps=[hfree] + all_stores)
            tq = S.dma("sp", lambda e: e.dma_start(out=qq[64:128, hs, 1, :], in_=q2_scr[h, :, :]), sem_h[hi])
            tq = S.dma("sp", lambda e: e.dma_start(out=kh[:, hs, :], in_=k_scr[h, :, :]), sem_h[hi])
            tq = S.dma("sp", lambda e: e.dma_start(out=vraw[:, hs, :, :],
                                                   in_=v_scr[:, h * 128:(h + 1) * 128].rearrange("(b p) d -> p b d", p=128)), sem_h[hi])
            tv = S.op("dve", lambda e: e.tensor_copy(out=vh[:, hs, :, 0:128], in_=vraw[:, hs, :, :]), deps=[tq, t_init, hfree])
            head_state[h] = dict(hi=hi, hs=hs, tv=tv)

        def phase_A(tk):
            h, i, grp = tk["h"], tk["i"], tk["grp"]
            if h not in head_state:
                head_prologue(h)
            hs, tv = head_state[h]["hs"], head_state[h]["tv"]
            ng = len(grp)
            si, sbank, sfree = sring.get()
            lastS = None
            for gi, (kb, oth, diag) in enumerate(grp):
                ps = psum[:, sbank, gi * 256:gi * 256 + 256]
                lastS = S.op("pe", lambda e: e.matmul(ps, lhsT=kh[:, hs, kb * 128:(kb + 1) * 128], rhs=qq[:, hs, :, i * 128:(i + 1) * 128],
                                                      start=True, stop=not diag),
                             deps=[tv, sfree, t_init], signal=not diag)
                if diag:
                    lastS = S.op("pe", lambda e: e.matmul(ps, lhsT=identb, rhs=trimask2, start=False, stop=True), deps=[t_c])
            pi, pslot, pfree = pring.get()
            bias = flags_s[:, 0:1] if grp[0][1] else 0.0
            nn = ng * 256
            pa = ptl[:, pslot, 0:nn]
            sview = psum[:, sbank, 0:nn]
            e1 = S.op("act", lambda e: e.activation(out=pa, in_=sview, func=AF.Exp, scale=0.125, bias=bias), deps=[lastS, pfree, t_c])
            e2 = e1
            sring.release(si, e2)
            tk.update(pi=pi, pslot=pslot, e1=e1, e2=e2)

        def phase_B(tk):
            h, i, grp = tk["h"], tk["i"], tk["grp"]
            hs, tv = head_state[h]["hs"], head_state[h]["tv"]
            if tk["first"]:
                oi, obank, ofree = oring.get()
                qb_state[(h, i)] = dict(oi=oi, obank=obank, ofree=ofree)
            qs = qb_state[(h, i)]
            O = psum[:, qs["obank"], 0:258].rearrange("p (a b) -> p a b", a=2)
            pslot = tk["pslot"]
            lastPV = None
            for gi, (kb, oth, diag) in enumerate(grp):
                pt, ee = ptl, tk["e1"]
                for a in (0, 1):
                    st_ = tk["first"] and gi == 0 and a == 0
                    fin = tk["last"] and gi == len(grp) - 1
                    c0_ = gi * 256 + a * 128
                    lastPV = S.op("pe", lambda e: e.matmul(O[:, a, :], lhsT=pt[:, pslot, c0_:c0_ + 128], rhs=vh[:, hs, kb, :],
                                                           start=st_, stop=fin, skip_group_check=True),
                                  deps=[ee, qs["ofree"], tv], signal=(gi == len(grp) - 1 and a == 1))
            pring.release(tk["pi"], lastPV)
            head_state[h]["last_pe"] = lastPV
            if not tk["last"]:
                return
            ei, es_, efree = ering.get()
            k0 = (eidx[0] % 32) * 8
            eidx[0] += 1
            sc = st[:, k0:k0 + 8]
            t1 = S.op("dve", lambda e: e.reciprocal(out=sc[:, 0:2], in_=O[:, :, 128]), deps=[lastPV, efree])
            t2 = S.op("dve", lambda e: e.tensor_tensor(out=sc[:, 2:3], in0=sc[:, 1:2], in1=lamw[:, 4:5], op=ALU.mult), deps=[t1, t_consts])
            o1 = o1n[:, es_, :]
            t3 = S.op("dve", lambda e: e.tensor_scalar(out=o1, in0=O[:, 0, 0:128], scalar1=sc[:, 0:1], scalar2=None, op0=ALU.mult), deps=[t1])
            ob = osb[:, es_, :]
            t4 = S.op("dve", lambda e: e.scalar_tensor_tensor(out=ob, in0=O[:, 1, 0:128], scalar=sc[:, 2:3], in1=o1,
                                                              op0=ALU.mult, op1=ALU.add), deps=[t2, t3])
            oring.release(qs["oi"], t4)
            t5 = S.op("act", lambda e: e.activation(out=junk4, in_=ob, func=AF.Square, accum_out=sc[:, 3:4]), deps=[t4])
            t6 = S.op("act", lambda e: e.activation(out=sc[:, 4:5], in_=sc[:, 3:4], func=AF.Ln, scale=1.0 / 128, bias=epsb), deps=[t5])
            t7 = S.op("act", lambda e: e.activation(out=sc[:, 5:6], in_=sc[:, 4:5], func=AF.Exp, scale=-0.5), deps=[t6])
            on = onb[:, es_, :]
            t8 = S.op("dve", lambda e: e.scalar_tensor_tensor(out=on, in0=ob, scalar=sc[:, 5:6], in1=gsub,
                                                              op0=ALU.mult, op1=ALU.mult), deps=[t7])
            pending_T.append(dict(h=h, i=i, on=on, t8=t8, ei=ei, age=0))
            if tk["i"] == NB - 1:
                hring.release(head_state[h]["hi"], lastPV)

        def flush_T(min_age):
            keep = []
            for pt_ in pending_T:
                if pt_["age"] < min_age:
                    pt_["age"] += 1
                    keep.append(pt_)
                    continue
                ti, tbank, tfree = tring4.get()
                tps = psum[:, tbank, 0:64].bitcast(BF16)
                on, h, i = pt_["on"], pt_["h"], pt_["i"]
                t9 = S.op("pe", lambda e: e.transpose(out=tps, in_=on, identity=identb), deps=[pt_["t8"], tfree])
                t10 = S.op("dve", lambda e: e.tensor_copy(out=catT[:, h, i * 128:(i + 1) * 128], in_=tps), deps=[t9])
                tring4.release(ti, t10)
                ering.release(pt_["ei"], (t9, pt_["t8"]))
            pending_T[:] = keep

        n_t = len(tasks)
        ADA_ATT = NGRP
        ada_every = max(1, n_t // max(1, n_ktiles * (ADA_ATT - ada_next[0])))
        for t_i in range(n_t + 2):
            if t_i == min(6, n_t):
                load_y_chunks()
            if t_i < n_t:
                phase_A(tasks[t_i])
            if t_i >= 2:
                phase_B(tasks[t_i - 2])
                flush_T(3)
            if t_i % ada_every == ada_every - 1:
                ada_step(ADA_ATT)
        flush_T(0)
        flush_T(0)
        while ada_next[0] < ADA_ATT:
            ada_more(1)
        S.op("dve", lambda e: e.scalar_tensor_tensor(out=gm2, in0=modT[:, 4 * KC:5 * KC], scalar=1.0, in1=n2g_s,
                                                     op0=ALU.add, op1=ALU.mult), deps=[ada_done[0], t_c])
        t_ycat = (s_y2, S.dcnt[s_y2])
        t_cat = (S.last("act"), S.last("dve"), t_ycat)
        S.barrier()

        if STOP == 4:
            S.barrier()
            raise _StopBuild()
        car.off = off4
        g1bc = car.f32([D])
        xcols = car.f32([2, NB, WNC])
        x1st = car.f32([2, NB, WNC])
        s_g = newsem("g")
        t_g1 = ld(g1bc, mod_scr[2 * D:3 * D].partition_broadcast(128), sem=s_g, deps=[t_modscr1])
        xr = Ring([0, 1])
        sem_xc = [newsem("xc"), newsem("xc")]
        sem_x1 = [newsem("x1"), newsem("x1")]
        pr = Ring([0, 1, 2, 3])
        x1_stores = []
        for c0 in range(0, D, WNC):
            wt, wtok, wi = wload(wsrc(w_out, 0, KC, c0, WNC), KC, WNC)
            xi, xslot, xfree = xr.get()
            tx = S.dma("sp", lambda e, xslot=xslot, c0=c0: e.dma_start(out=xcols[:, xslot, :, :],
                                                                      in_=xin[0:TOWN, c0:c0 + WNC].rearrange("(b p) n -> p b n", p=128)),
                       sem_xc[xi], deps=[xfree])
            tl = []
            for tb in range(NB):
                bi, bank, bfree = pr.get()
                ps = psum[:, bank, 0:WNC]
                last = None
                for k in range(KC):
                    last = S.op("pe", lambda e, ps=ps, k=k, tb=tb: e.matmul(ps, lhsT=catT[:, k, tb * 128:(tb + 1) * 128], rhs=wt[:, k, :],
                                                                            start=(k == 0), stop=(k == KC - 1)),
                                deps=[wtok, bfree, t_cat] if k == 0 else (), signal=(k == KC - 1))
                dst = x1st[:, xslot, tb, :]
                t1 = S.op("dve", lambda e, dst=dst, ps=ps, c0=c0: e.tensor_tensor(out=dst, in0=ps, in1=g1bc[:, c0:c0 + WNC], op=ALU.mult),
                          deps=[last, t_g1, xfree])
                pr.release(bi, t1)
                t2 = S.op("dve", lambda e, dst=dst, xslot=xslot, tb=tb: e.tensor_tensor(out=dst, in0=dst, in1=xcols[:, xslot, tb, :], op=ALU.add),
                          deps=[t1, tx])
                tl.append(t2)
            wring.release(wi, S.last("pe"))
            d = S.dma("sp", lambda e, xslot=xslot, c0=c0: e.dma_start(out=x1_scr[:, c0:c0 + WNC].rearrange("(b p) n -> p b n", p=128),
                                                                     in_=x1st[:, xslot, :, :]), sem_x1[xi], deps=tl)
            xr.release(xi, d)
            x1_stores.append(d)
            if (c0 // WNC) % 2 == 1:
                ada_more(1)
        while ada_next[0] < NGRP:
            ada_more(1)
        t_modscr = S.dma("sp", lambda e: e.dma_start(out=mod_scr_T[:, 5 * KC:6 * KC], in_=modT[:, 5 * KC:6 * KC]), s_m, deps=[ada_done[0]])
        S.barrier()

        if STOP == 5:
            S.barrier()
            raise _StopBuild()
        car.off = 0
        h2T = car.bf([KC, PASS])
        x2 = car.f32([PB, D])
        g2bc = car.f32([D])
        off7 = car.off
        nparts = 8 if FC >= 8 else 2
        base, rem = divmod(FC, nparts)
        parts = [base + (1 if i < rem else 0) for i in range(nparts)]
        PMAX = max(parts)
        DNC = 512 if (PMAX * 512 <= cfg.WCAP and D % 512 == 0) else WNC
        s_fg = newsem("fg")
        sem_x2 = [newsem("x2") for _ in range(PB)]
        sem_o = newsem("o")
        out_toks = []
        t_prev_pass = None
        for p in range(NPASS):
            r0 = p * PASS
            c = Carver()
            c.off = off7
            junk7 = c.bf([D])
            xn6 = c.f32([D])
            xn6_free = [None]
            t_g2 = ld(g2bc, mod_scr[5 * D:6 * D].partition_broadcast(128), sem=s_g, deps=[t_modscr, t_prev_pass])
            tx2 = []
            for tb in range(PB):
                tx2.append(S.dma("sp", lambda e: e.dma_start(out=x2[:, tb, :], in_=x1_scr[r0 + tb * 128:r0 + (tb + 1) * 128, :]),
                                 sem_x2[tb], deps=x1_stores + [t_prev_pass]))
            pr6 = Ring([0, 1, 2, 3])

            def stats6(tb):
                k0 = (tb % 8) * 4
                xv = x2[:, tb, :]
                t1 = S.op("act", lambda e: e.activation(out=junk7, in_=xv, func=AF.Square, accum_out=small[:, k0:k0 + 1]),
                          deps=[tx2[tb], t_prev_pass])
                t2 = S.op("act", lambda e: e.activation(out=small[:, k0 + 1:k0 + 2], in_=small[:, k0:k0 + 1], func=AF.Sqrt,
                                                        scale=1.0 / D, bias=epsb), deps=[t1])
                t3 = S.op("dve", lambda e: e.reciprocal(out=small[:, k0 + 2:k0 + 3], in_=small[:, k0 + 1:k0 + 2]), deps=[t2])
                t4 = S.op("act", lambda e: e.activation(out=xn6, in_=xv, func=AF.Copy, scale=small[:, k0 + 2:k0 + 3]),
                          deps=[t3, xn6_free[0]])
                return dict(t4=t4)

            def mm_evac6(tb, sb):
                for k4 in range(0, KC, 4):
                    bi, bank, bfree = pr6.get()
                    tp = None
                    for q in range(4):
                        k = k4 + q
                        tp = S.op("pe", lambda e: e.transpose(out=psum[:, bank, q * 128:(q + 1) * 128], in_=xn6[:, k * 128:(k + 1) * 128], identity=identf),
                                  deps=[sb["t4"], bfree, t_prev_pass, t_c] if q == 0 else (), signal=(q == 3))
                    eng = "act" if ((k4 // 4) % 8) == 0 else "dve"
                    lastev = None
                    for q in range(4):
                        k = k4 + q
                        dst = h2T[:, k, tb * 128:(tb + 1) * 128]
                        src = psum[:, bank, q * 128:(q + 1) * 128]
                        if eng == "act":
                            lastev = S.op("act", lambda e: e.activation(out=dst, in_=src, func=AF.Identity, scale=gm2[:, k:k + 1],
                                                                        bias=modT[:, 3 * KC + k:3 * KC + k + 1]), deps=[tp])
                        else:
                            lastev = S.op("dve", lambda e: e.tensor_scalar(out=dst, in0=src, scalar1=gm2[:, k:k + 1],
                                                                           scalar2=modT[:, 3 * KC + k:3 * KC + k + 1], op0=ALU.mult, op1=ALU.add), deps=[tp])
                    pr6.release(bi, lastev)
                xn6_free[0] = S.last("pe")

            for tb in range(PB):
                mm_evac6(tb, stats6(tb))
            t_h = (S.last("act"), S.last("dve"))
            S.barrier()
            c = Carver()
            c.off = off7
            gT = c.bf([PMAX, PASS])
            sg = c.f32([2, PASS])
            dtmp = c.f32([2, DNC])
            gring = Ring([0, 1, 2, 3])
            dring = Ring([4, 5])
            sgr = Ring([0, 1])
            dtr = Ring([0, 1])
            f0 = 0
            t_down_last = None
            t_x2_last = None
            for pi_, nf in enumerate(parts):
                tg_l = []
                fl = 0
                while fl < nf:
                    ncol = min(WNC, (nf - fl) * 128)
                    ncc = ncol // 128
                    col0 = (f0 + fl) * 128
                    wg, wgtok, wgi = wload(wsrc(w_gate, 0, KC, col0, ncol), KC, ncol)
                    sgs = []
                    for cc in range(ncc):
                        bi, bank, bfree = gring.get()
                        ps = psum[:, bank, 0:PASS]
                        last = None
                        for k in range(KC):
                            last = S.op("pe", lambda e: e.matmul(ps, lhsT=wg[:, k, cc * 128:(cc + 1) * 128], rhs=h2T[:, k, :],
                                                                 start=(k == 0), stop=(k == KC - 1)),
                                        deps=[wgtok, bfree, t_h, t_down_last] if k == 0 else (), signal=(k == KC - 1))
                        si, ss_, sfree = sgr.get()
                        sgt = sg[:, ss_, :]
                        t1 = S.op("act", lambda e: e.activation(out=sgt, in_=ps, func=AF.Silu), deps=[last, sfree])
                        gring.release(bi, t1)
                        sgs.append((si, sgt, t1))
                    wring.release(wgi, S.last("pe"))
                    wu, wutok, wui = wload(wsrc(w_up, 0, KC, col0, ncol), KC, ncol)
                    for cc in range(ncc):
                        bi, bank, bfree = gring.get()
                        ps = psum[:, bank, 0:PASS]
                        last = None
                        for k in range(KC):
                            last = S.op("pe", lambda e: e.matmul(ps, lhsT=wu[:, k, cc * 128:(cc + 1) * 128], rhs=h2T[:, k, :],
                                                                 start=(k == 0), stop=(k == KC - 1)),
                                        deps=[wutok, bfree] if k == 0 else (), signal=(k == KC - 1))
                        si, sgt, ts = sgs[cc]
                        dst = gT[:, fl + cc, :]
                        t2 = S.op("dve", lambda e: e.tensor_tensor(out=dst, in0=ps, in1=sgt, op=ALU.mult), deps=[last, ts, t_down_last])
                        gring.release(bi, t2)
                        sgr.release(si, t2)
                        tg_l.append(t2)
                    wring.release(wui, S.last("pe"))
                    fl += ncc
                for c0 in range(0, D, DNC):
                    wd, wdtok, wdi = wload(wsrc(w_down, f0 * 128, nf, c0, DNC), nf, DNC)
                    for tb in range(PB):
                        bi, bank, bfree = dring.get()
                        ps = psum[:, bank, 0:DNC]
                        last = None
                        for k in range(nf):
                            last = S.op("pe", lambda e: e.matmul(ps, lhsT=gT[:, k, tb * 128:(tb + 1) * 128], rhs=wd[:, k, :],
                                                                 start=(k == 0), stop=(k == nf - 1)),
                                        deps=[wdtok, bfree] + tg_l if k == 0 else (), signal=(k == nf - 1))
                        di, ds, dfree = dtr.get()
                        dt_ = dtmp[:, ds, :]
                        t1 = S.op("dve", lambda e: e.tensor_tensor(out=dt_, in0=ps, in1=g2bc[:, c0:c0 + DNC], op=ALU.mult),
                                  deps=[last, dfree, t_g2])
                        dring.release(bi, t1)
                        xv = x2[:, tb, c0:c0 + DNC]
                        t2 = S.op("dve", lambda e: e.tensor_tensor(out=xv, in0=xv, in1=dt_, op=ALU.add), deps=[t1, tx2[tb]])
                        dtr.release(di, t2)
                        t_x2_last = t2
                    wring.release(wdi, S.last("pe"))
                t_down_last = S.last("pe")
                f0 += nf
            tfg = ld(g2bc, fgv.partition_broadcast(128), sem=s_fg, deps=[t_x2_last])
            d = None
            for tb in range(PB):
                k0 = (tb % 8) * 4
                xv = x2[:, tb, :]
                t1 = S.op("act", lambda e: e.activation(out=h2T.rearrange("p a b -> p (a b)")[:, 0:D], in_=xv, func=AF.Square,
                                                        accum_out=small[:, k0:k0 + 1]), deps=[t_x2_last, t_down_last])
                t2 = S.op("act", lambda e: e.activation(out=small[:, k0 + 1:k0 + 2], in_=small[:, k0:k0 + 1], func=AF.Sqrt,
                                                        scale=1.0 / D, bias=epsb), deps=[t1])
                t3 = S.op("dve", lambda e: e.reciprocal(out=small[:, k0 + 2:k0 + 3], in_=small[:, k0 + 1:k0 + 2]), deps=[t2])
                t4 = S.op("dve", lambda e: e.scalar_tensor_tensor(out=xv, in0=xv, scalar=small[:, k0 + 2:k0 + 3], in1=g2bc,
                                                                  op0=ALU.mult, op1=ALU.mult), deps=[t3, tfg])
                d = S.dma("sp", lambda e: e.dma_start(out=out[r0 + tb * 128:r0 + (tb + 1) * 128, :], in_=xv), sem_o, deps=[t4])
                out_toks.append(d)
            t_prev_pass = (d, S.last("dve"), S.last("act"), S.last("pe"))
            S.barrier()

    except _StopBuild:
        pass
    S.ops["sp"].append((S._waits("sp", out_toks), None, None, 0, 0))

    with nc.allow_non_contiguous_dma(reason="tiny transposed store of modulation vectors"):
        with nc.Block() as block:
            S.emit(block, sems)
    es.close()
    return nc


def host_consts(cfg, half):
    bf = ml_dtypes.bfloat16
    TOWN, TALL = cfg.TOWN, cfg.TALL
    own0 = half * TOWN
    oth0 = (1 - half) * TOWN
    pos = np.concatenate([np.arange(own0, own0 + TOWN), np.arange(oth0, oth0 + TOWN)]).astype(np.float32)
    inv = (ROPE_THETA ** (-np.arange(32, dtype=np.float32) / 32)).astype(np.float32)
    ang = pos[None, :] * inv[:, None]
    cos = np.cos(ang).astype(np.float32)
    sin = np.sin(ang).astype(np.float32)
    cosT = np.tile(cos, (4, 1))
    sinsT = np.concatenate([-sin, sin, -sin, sin], axis=0)
    ident = np.eye(128, dtype=np.float32)
    perm = np.zeros((128, 128), np.float32)
    for m in range(128):
        perm[m ^ 32, m] = 1.0
    p = np.arange(128)[:, None]
    cidx = np.arange(128)[None, :]
    tri = np.where(p <= cidx, 0.0, NEG).astype(np.float32)
    flags = np.zeros((128, 2), np.float32)
    flags[:, 0] = 0.0 if half == 1 else NEG
    flags[:, 1] = 1.0 if half == 1 else 0.0
    return dict(cosT=np.ascontiguousarray(cosT), sinsT=np.ascontiguousarray(sinsT), identf=ident, identb=ident.astype(bf),
                permb=perm.astype(bf), trimask=np.concatenate([tri, tri], axis=1).astype(bf), flags=flags)


def make_in_maps(cfg, x, c, w_ada, b_ada, norm1_g, w_in, lambda_q1, lambda_k1, lambda_q2, lambda_k2, subln_g, conv_w, w_out,
                 norm2_g, w_gate, w_up, w_down, final_g, n_cores):
    f = np.float32
    D, KC, TOWN = cfg.D, cfg.KC, cfg.TOWN

    def colT(v, nchunk):
        return np.ascontiguousarray(np.asarray(v, f).reshape(nchunk, 128).T)

    shared = dict(
        w_ada=np.ascontiguousarray(np.asarray(w_ada[0], f)), badaT=colT(b_ada[0], 6 * KC),
        n1gT=colT(norm1_g[0], KC), n2gT=colT(norm2_g[0], KC), fgv=np.ascontiguousarray(np.asarray(final_g, f)),
        w_in=np.ascontiguousarray(np.asarray(w_in[0], f)), w_out=np.ascontiguousarray(np.asarray(w_out[0], f)),
        w_gate=np.ascontiguousarray(np.asarray(w_gate[0], f)), w_up=np.ascontiguousarray(np.asarray(w_up[0], f)),
        w_down=np.ascontiguousarray(np.asarray(w_down[0], f)),
        lamv=np.concatenate([np.asarray(v[0], f) for v in (lambda_q1, lambda_k1, lambda_q2, lambda_k2)]),
        sublng=np.ascontiguousarray(np.asarray(subln_g[0], f)),
        convwT=np.ascontiguousarray(np.asarray(conv_w[0], f).reshape(3, cfg.CC, 128).transpose(2, 1, 0).reshape(128, cfg.CC * 3)),
    )
    hc = [host_consts(cfg, 0), host_consts(cfg, 1)]
    maps = []
    x = np.asarray(x, f)
    c = np.asarray(c, f)
    for core in range(n_cores):
        b, half = core // 2, core % 2
        xb = x[b]
        xin = np.concatenate([xb[half * TOWN:(half + 1) * TOWN], xb[(1 - half) * TOWN:(2 - half) * TOWN]], axis=0)
        m = dict(shared)
        m.update(hc[half])
        m["xin"] = np.ascontiguousarray(xin)
        m["cT"] = colT(c[b], KC)
        maps.append(m)
    return maps


_NC_CACHE = {}


def kernel(**inputs):
    cfg = Cfg()
    if "nc" not in _NC_CACHE:
        _NC_CACHE["nc"] = build(cfg)
    nc = _NC_CACHE["nc"]
    maps = make_in_maps(cfg, n_cores=8, **inputs)
    res = run_bass_kernel_spmd(nc, maps, core_ids=list(range(8)))
    B = inputs["x"].shape[0]
    outp = np.empty((B, cfg.S, cfg.D), np.float32)
    for core in range(8):
        b, half = core // 2, core % 2
        outp[b, half * cfg.TOWN:(half + 1) * cfg.TOWN] = res.results[core]["out"]
    return outp
```

```python
from contextlib import ExitStack

import sys
import ml_dtypes
import numpy as np

import concourse.bass as bass
import concourse.mybir as mybir
from concourse.bass_utils import run_bass_kernel_spmd

F32 = mybir.dt.float32
BF16 = mybir.dt.bfloat16
AF = mybir.ActivationFunctionType
ALU = mybir.AluOpType
AX = mybir.AxisListType

NEG = -30000.0
DEBUG_W = False
STOP = 99


class _StopBuild(Exception):
    pass
LINEMAP = {}
NORM_EPS = 1e-6
ROPE_THETA = 10000.0
LAM_INIT = 0.8 - 0.6 * 1.0


class Cfg:
    def __init__(self, D=4096, S=2048, H=16, FF=11008):
        self.D, self.S, self.H, self.FF = D, S, H, FF
        self.KC = D // 128
        self.TOWN = S // 2
        self.TALL = S
        self.NB = self.TOWN // 128
        self.AW = H * 128
        self.QKW = H * 128
        self.CW = D - self.AW
        self.CC = self.CW // 128
        self.INW = 2 * self.QKW + self.AW + 3 * self.CW
        self.FC = FF // 128
        self.TG = min(512, self.TOWN)
        self.PASS = min(512, self.TOWN)
        self.WCAP = 8192
        self.NWS = 3


class _Rec:
    def __init__(self):
        self.call = None

    def __getattr__(self, name):
        def f(*a, **k):
            assert self.call is None
            self.call = (name, a, k)
            return self
        return f


def _record(fn):
    if fn is None:
        return None
    r = _Rec()
    fn(r)
    assert r.call is not None
    return r.call


class Sched:
    ENG = ("pe", "act", "dve", "pool", "sp")

    def __init__(self):
        self.ops = {e: [] for e in self.ENG}
        self.cnt = {e: 0 for e in self.ENG}
        self.seen = {e: {} for e in self.ENG}
        self.dcnt = {}

    @staticmethod
    def _flat(deps, outl):
        if deps is None:
            return outl
        if isinstance(deps, tuple) and len(deps) == 2 and isinstance(deps[0], str) and isinstance(deps[1], int):
            outl.append(deps)
            return outl
        for d in deps:
            Sched._flat(d, outl)
        return outl

    def _waits(self, eng, deps):
        w = []
        for d in self._flat(deps, []):
            key, val = d
            if self.seen[eng].get(key, 0) >= val:
                continue
            self.seen[eng][key] = val
            w.append((key, val))
        return w

    def op(self, eng, fn, deps=(), signal=True):
        w = self._waits(eng, deps)
        tok = None
        if signal:
            self.cnt[eng] += 1
            tok = (eng, self.cnt[eng])
        self.ops[eng].append((w, _record(fn), eng if signal else None, 1, sys._getframe(1).f_lineno))
        return tok

    def dma(self, eng, fn, sem, deps=()):
        w = self._waits(eng, deps)
        self.dcnt[sem] = self.dcnt.get(sem, 0) + 16
        self.ops[eng].append((w, _record(fn), sem, 16, sys._getframe(1).f_lineno))
        return (sem, self.dcnt[sem])

    def last(self, eng):
        return (eng, self.cnt[eng]) if self.cnt[eng] else None

    def barrier(self, engines=("pe", "act", "dve", "sp")):
        toks = [self.last(e) for e in engines]
        toks += [(s, v) for s, v in self.dcnt.items()]
        for e in engines:
            w = self._waits(e, toks)
            if w:
                self.ops[e].append((w, None, None, 0, 0))

    def emit(self, block, sems):
        def run(engobj, name):
            for (w, fn, inc, amt, srcl) in self.ops[name]:
                for key, val in w:
                    engobj.wait_ge(sems[key], val)
                if fn is None:
                    continue
                name_, a_, k_ = fn
                ins = getattr(engobj, name_)(*a_, **k_)
                LINEMAP[ins.ins.name] = srcl
                if inc is not None:
                    ins.then_inc(sems[inc], amt)

        @block.tensor
        def _(e):
            run(e, "pe")

        @block.scalar
        def _(e):
            run(e, "act")

        @block.vector
        def _(e):
            run(e, "dve")

        @block.gpsimd
        def _(e):
            run(e, "pool")

        @block.sync
        def _(e):
            run(e, "sp")


class Ring:
    def __init__(self, items):
        self.items = list(items)
        self.free = [None] * len(self.items)
        self.i = 0

    def get(self):
        i = self.i
        self.i = (i + 1) % len(self.items)
        return i, self.items[i], self.free[i]

    def release(self, i, tok):
        self.free[i] = tok


def build(cfg):
    nc = bass.Bass("TRN2", target_bir_lowering=False)
    D, KC, TOWN, TALL, NB, H = cfg.D, cfg.KC, cfg.TOWN, cfg.TALL, cfg.NB, cfg.H
    QKW, AW, CW, CC, INW, FF, FC, TG, PASS = cfg.QKW, cfg.AW, cfg.CW, cfg.CC, cfg.INW, cfg.FF, cfg.FC, cfg.TG, cfg.PASS
    NG_OWN = TOWN // TG
    NPASS = TOWN // PASS
    PB = PASS // 128

    def din(name, shape, dt=F32):
        return nc.dram_tensor(name, list(shape), dt, kind="ExternalInput").ap()

    xin = din("xin", [TALL, D])
    cT = din("cT", [128, KC])
    w_ada = din("w_ada", [D, 6 * D])
    badaT = din("badaT", [128, 6 * KC])
    n1gT = din("n1gT", [128, KC])
    n2gT = din("n2gT", [128, KC])
    fgv = din("fgv", [D])
    w_in = din("w_in", [D, INW])
    w_out = din("w_out", [D, D])
    w_gate = din("w_gate", [D, FF])
    w_up = din("w_up", [D, FF])
    w_down = din("w_down", [FF, D])
    lamv = din("lamv", [4 * 64])
    sublng = din("sublng", [128])
    convwT = din("convwT", [128, CC * 3])
    cosT = din("cosT", [128, TALL])
    sinsT = din("sinsT", [128, TALL])
    identf_d = din("identf", [128, 128])
    identb_d = din("identb", [128, 128], BF16)
    permb_d = din("permb", [128, 128], BF16)
    trimask_d = din("trimask", [128, 256], BF16)
    flags_d = din("flags", [128, 2])
    out = nc.dram_tensor("out", [TOWN, D], F32, kind="ExternalOutput").ap()

    q1_scr = nc.dram_tensor("q1_scr", [H, 64, TOWN], BF16).ap()
    q2_scr = nc.dram_tensor("q2_scr", [H, 64, TOWN], BF16).ap()
    k_scr = nc.dram_tensor("k_scr", [H, 128, TALL], BF16).ap()
    v_scr = nc.dram_tensor("v_scr", [TALL, AW], BF16).ap()
    y_scr = nc.dram_tensor("y_scr", [CC, 128, TOWN], BF16).ap()
    x1_scr = nc.dram_tensor("x1_scr", [TOWN, D], F32).ap()
    mod_scr = nc.dram_tensor("mod_scr", [6 * D], F32).ap()

    S = Sched()
    es = ExitStack()
    sems = {}

    def mksem(name):
        sems[name] = es.enter_context(nc.semaphore(name))
        return name

    for e in Sched.ENG:
        mksem(e)

    ARENA = 72 * 1024
    arena = es.enter_context(nc.sbuf_tensor("arena", [128, ARENA], BF16))
    wpool = es.enter_context(nc.sbuf_tensor("wpool", [128, cfg.NWS, cfg.WCAP], BF16))
    consts = es.enter_context(nc.sbuf_tensor("consts", [128, 2048], F32))
    constb = es.enter_context(nc.sbuf_tensor("constb", [128, 512], BF16))
    psum = es.enter_context(nc.psum_tensor("psum", [128, 8, 512], F32))

    class Carver:
        def __init__(self):
            self.off = 0

        def take(self, n_bf16):
            o = self.off
            self.off += (n_bf16 + 15) // 16 * 16
            assert self.off <= ARENA, (self.off, ARENA)
            return o

        def bf(self, shape):
            n = int(np.prod(shape))
            o = self.take(n)
            ap = arena[:, o:o + n]
            return self._shape(ap, shape)

        def f32(self, shape):
            n = int(np.prod(shape))
            o = self.take(2 * n)
            ap = arena[:, o:o + 2 * n].bitcast(F32)
            return self._shape(ap, shape)

        @staticmethod
        def _shape(ap, shape):
            if len(shape) == 1:
                return ap
            if len(shape) == 2:
                return ap.rearrange("p (a b) -> p a b", a=shape[0])
            if len(shape) == 3:
                return ap.rearrange("p (a b c) -> p a b c", a=shape[0], b=shape[1])
            raise ValueError(shape)

    cpos = [0]

    def ctake(n):
        o = cpos[0]
        cpos[0] += n
        assert cpos[0] <= 2048
        return consts[:, o:o + n]

    identf = ctake(128)
    modT = ctake(6 * KC)
    badaT_s = ctake(6 * KC)
    n1g_s = ctake(KC)
    n2g_s = ctake(KC)
    gm1 = ctake(KC)
    gm2 = ctake(KC)
    cact = ctake(KC)
    convw_s = ctake(CC * 3)
    flags_s = ctake(2)
    lam_s = ctake(256)
    lamw = ctake(16)
    gsub = ctake(128)
    epsb = ctake(1)
    small = ctake(64 + 32 * 8)
    bpos = [0]

    def btake(n):
        o = bpos[0]
        bpos[0] += n
        assert bpos[0] <= 512
        return constb[:, o:o + n]

    identb = btake(128)
    permb = btake(128)
    trimask2 = btake(256)

    wring = Ring(range(cfg.NWS))
    for i in range(cfg.NWS):
        mksem(f"w{i}")

    def wload(src_ap, kc, ncols):
        assert kc * ncols <= cfg.WCAP
        i, slot, free = wring.get()
        if DEBUG_W: print('wload', i, free, src_ap.tensor.name, S.last('pe'))
        dst = wpool[:, slot, 0:kc * ncols].rearrange("p (k n) -> p k n", k=kc)
        tok = S.dma("pool", lambda e, dst=dst, src=src_ap: e.dma_start(out=dst, in_=src), f"w{i}", deps=[free])
        return dst, tok, i

    def wsrc(w, r0, kc, c0, ncols):
        return w[r0:r0 + kc * 128, c0:c0 + ncols].rearrange("(k p) n -> p k n", p=128)

    dsem_i = [0]

    def newsem(prefix):
        dsem_i[0] += 1
        return mksem(f"{prefix}{dsem_i[0]}")

    out_toks = []
    try:
        s_c = newsem("c")

        def ld(dst, src, eng="sp", sem=None, deps=()):
            return S.dma(eng, lambda e, dst=dst, src=src: e.dma_start(out=dst, in_=src), sem or s_c, deps=deps)

        ld(identf, identf_d)
        ld(badaT_s, badaT)
        ld(n1g_s, n1gT)
        ld(n2g_s, n2gT)
        ld(cact, cT)
        ld(convw_s, convwT)
        ld(flags_s, flags_d)
        ld(lam_s, lamv.partition_broadcast(128))
        ld(gsub, sublng.partition_broadcast(128))
        ld(identb, identb_d)
        ld(permb, permb_d)
        t_c = ld(trimask2, trimask_d)

        t = S.op("dve", lambda e: e.memset(epsb, NORM_EPS), deps=[t_c])
        t = S.op("dve", lambda e: e.tensor_tensor(out=lam_s[:, 0:64], in0=lam_s[:, 0:64], in1=lam_s[:, 64:128], op=ALU.mult), deps=[t])
        t = S.op("dve", lambda e: e.tensor_tensor(out=lam_s[:, 128:192], in0=lam_s[:, 128:192], in1=lam_s[:, 192:256], op=ALU.mult), deps=[t])
        t = S.op("dve", lambda e: e.tensor_reduce(out=lamw[:, 0:1], in_=lam_s[:, 0:64], axis=AX.X, op=ALU.add), deps=[t])
        t = S.op("dve", lambda e: e.tensor_reduce(out=lamw[:, 1:2], in_=lam_s[:, 128:192], axis=AX.X, op=ALU.add), deps=[t])
        t = S.op("act", lambda e: e.activation(out=lamw[:, 2:4], in_=lamw[:, 0:2], func=AF.Exp), deps=[t])
        t = S.op("dve", lambda e: e.tensor_tensor(out=lamw[:, 4:5], in0=lamw[:, 3:4], in1=lamw[:, 2:3], op=ALU.subtract), deps=[t])
        t = S.op("dve", lambda e: e.tensor_scalar(out=lamw[:, 4:5], in0=lamw[:, 4:5], scalar1=-LAM_INIT, scalar2=None, op0=ALU.add), deps=[t])
        t = S.op("dve", lambda e: e.tensor_scalar(out=gsub, in0=gsub, scalar1=1.0 - LAM_INIT, scalar2=None, op0=ALU.mult), deps=[t])
        t_consts = t

        if STOP == 0:
            S.barrier()
            raise _StopBuild()
        car = Carver()
        car.off = ARENA - (KC * 128 + 2 * 2 * 512)
        ADA_LIMIT = car.off
        cb = car.bf([KC, 128])
        modtmp = car.f32([2, 512])
        car.off = 0
        off_stage = car.off
        t_cb = S.op("act", lambda e: e.activation(out=cb, in_=cact.unsqueeze(2).to_broadcast([128, KC, 128]), func=AF.Silu),
                    deps=[t_c])
        pring_mod = Ring([7])
        AK = getattr(cfg, "AK", min(KC, 16))
        ANC = 512 if D >= 512 else D
        n_ktiles = KC // AK
        mtmp_ring = Ring([0, 1])

        ada_state = {}

        def ada_tile(j0):
            if not ada_state:
                bi, bank, bfree = pring_mod.get()
                ada_state.update(bi=bi, bank=bank, bfree=bfree, kt=0)
            st_ = ada_state
            kt = st_["kt"]
            ps = psum[:, st_["bank"], 0:ANC]
            wt, wtok, wi = wload(wsrc(w_ada, kt * AK * 128, AK, j0 * ANC, ANC), AK, ANC)
            last = None
            for k in range(AK):
                kk = kt * AK + k
                last = S.op("pe", lambda e: e.matmul(ps, lhsT=cb[:, kk, :], rhs=wt[:, k, :], start=(kk == 0), stop=(kk == KC - 1)),
                            deps=[wtok, t_cb, st_["bfree"]] if k == 0 else (), signal=(k == AK - 1))
            wring.release(wi, last)
            st_["kt"] = kt + 1
            if st_["kt"] < n_ktiles:
                return None
            nch = ANC // 128
            mi, ms, mfree = mtmp_ring.get()
            mt = modtmp[:, ms, 0:ANC].rearrange("p (a b) -> p a b", a=nch)
            t1 = S.op("dve", lambda e: e.tensor_tensor(out=mt, in0=ps.rearrange("p (a b) -> p a b", a=nch),
                                                       in1=identf.unsqueeze(1).to_broadcast([128, nch, 128]), op=ALU.mult),
                      deps=[last, mfree, t_c])
            pring_mod.release(st_["bi"], t1)
            c0 = j0 * nch
            t2 = S.op("dve", lambda e: e.tensor_reduce(out=modT[:, c0:c0 + nch], in_=mt, axis=AX.X, op=ALU.add), deps=[t1])
            t3 = S.op("dve", lambda e: e.tensor_tensor(out=modT[:, c0:c0 + nch], in0=modT[:, c0:c0 + nch],
                                                       in1=badaT_s[:, c0:c0 + nch], op=ALU.add), deps=[t2])
            mtmp_ring.release(mi, t3)
            ada_state.clear()
            return t3

        def ada_group(j0):
            t_ = None
            while t_ is None:
                t_ = ada_tile(j0)
            return t_

        NGRP = 6 * D // ANC
        GPM = D // ANC
        t = None
        for j in range(2 * GPM):
            t = ada_group(j)
        t = S.op("dve", lambda e: e.scalar_tensor_tensor(out=gm1, in0=modT[:, KC:2 * KC], scalar=1.0, in1=n1g_s,
                                                         op0=ALU.add, op1=ALU.mult), deps=[t, t_c])
        t_gm1 = t
        ada_next = [2 * GPM]
        ada_done = [None]

        def ada_step(limit):
            if ada_next[0] < min(limit, NGRP):
                t_ = ada_tile(ada_next[0])
                if t_ is not None:
                    ada_done[0] = t_
                    ada_next[0] += 1

        def ada_more(n=1):
            for _ in range(n):
                if ada_next[0] < NGRP:
                    ada_done[0] = ada_group(ada_next[0])
                    ada_next[0] += 1

        if STOP == 1:
            S.barrier()
            raise _StopBuild()
        def norm_transpose(car0, src_rows, nblk, dstT, gm, shift, t_dst_free, t_src=None, extra_deps=(), nslots=3):
            c = Carver()
            c.off = car0
            xs = c.f32([nslots, D])
            junk = c.bf([D])
            xring = Ring(range(nslots))
            sems_x = [newsem("x") for _ in range(nslots)]
            pr = Ring([0, 1, 2, 3, 4, 5])
            stat = small
            def stats(b):
                xi, xslot, xfree = xring.get()
                xb = xs[:, xslot, :]
                tl = S.dma("sp", lambda e: e.dma_start(out=xb, in_=src_rows[b * 128:(b + 1) * 128, :]),
                           sems_x[xi], deps=[xfree, t_src, t_dst_free] + list(extra_deps))
                c0 = (b % 8) * 4
                t1 = S.op("act", lambda e: e.activation(out=junk, in_=xb, func=AF.Square, accum_out=stat[:, c0:c0 + 1]), deps=[tl])
                t2 = S.op("act", lambda e: e.activation(out=stat[:, c0 + 1:c0 + 2], in_=stat[:, c0:c0 + 1], func=AF.Sqrt,
                                                        scale=1.0 / D, bias=epsb), deps=[t1, t_consts])
                t3 = S.op("dve", lambda e: e.reciprocal(out=stat[:, c0 + 2:c0 + 3], in_=stat[:, c0 + 1:c0 + 2]), deps=[t2])
                t4 = S.op("act", lambda e: e.activation(out=xb, in_=xb, func=AF.Copy, scale=stat[:, c0 + 2:c0 + 3]), deps=[t3])
                return dict(xi=xi, xb=xb, tl=tl, t4=t4)

            def mm_evac(b, sb):
                xb = sb["xb"]
                lastev = None
                for k4 in range(0, KC, 4):
                    bi, bank, bfree = pr.get()
                    tp = None
                    for q in range(4):
                        k = k4 + q
                        tp = S.op("pe", lambda e: e.transpose(out=psum[:, bank, q * 128:(q + 1) * 128], in_=xb[:, k * 128:(k + 1) * 128], identity=identf),
                                  deps=[sb["t4"], sb["tl"], bfree, t_c] if q == 0 else (), signal=(q == 3))
                    eng = "act" if ((k4 // 4) % 8) == 0 else "dve"
                    for q in range(4):
                        k = k4 + q
                        dst = dstT[:, k, b * 128:(b + 1) * 128]
                        src = psum[:, bank, q * 128:(q + 1) * 128]
                        if eng == "act":
                            lastev = S.op("act", lambda e: e.activation(out=dst, in_=src, func=AF.Identity,
                                                                        scale=gm[:, k:k + 1], bias=shift[:, k:k + 1]), deps=[tp, t_gm1])
                        else:
                            lastev = S.op("dve", lambda e: e.tensor_scalar(out=dst, in0=src, scalar1=gm[:, k:k + 1],
                                                                           scalar2=shift[:, k:k + 1], op0=ALU.mult, op1=ALU.add), deps=[tp, t_gm1])
                    pr.release(bi, lastev)
                xring.release(sb["xi"], S.last("pe"))

            nxt = stats(0)
            for b in range(nblk):
                cur = nxt
                if b + 1 < nblk:
                    nxt = stats(b + 1)
                mm_evac(b, cur)
            return (S.last("act"), S.last("dve"))

        car.off = off_stage
        hT = car.bf([KC, TOWN])
        halo_h = car.bf([KC, 2])
        off3 = car.off

        QOFF, KOFF, VOFF = 0, QKW, 2 * QKW
        BOFF, COFF, XOFF = 2 * QKW + AW, 2 * QKW + AW + CW, 2 * QKW + AW + 2 * CW
        WNC = min(256, cfg.WCAP // KC)

        def in_proj(half):
            tok0 = TOWN if half == 0 else 0
            ntile = [0]

            def tick():
                ntile[0] += 1
                if ntile[0] % 2 == 0:
                    ada_step(4 * GPM)

            c = Carver()
            c.off = off3
            cos_s = c.f32([TOWN])
            sin_s = c.f32([TOWN])
            Tb = c.bf([2, TG])
            tmpf = c.f32([2, TG])
            qst = c.bf([2, TG])
            vst = c.bf([2, WNC])
            csb = c.f32([WNC // 128, TOWN + 2])
            usb = c.f32([WNC // 128, TOWN + 2])
            ycv = c.f32([WNC // 128, TOWN])
            yst = c.bf([WNC // 128, TOWN])
            s_t = newsem("t")
            t_tab = ld(cos_s, cosT[:, tok0:tok0 + TOWN], sem=s_t, deps=[S.last("dve"), S.last("act")])
            t_tab = ld(sin_s, sinsT[:, tok0:tok0 + TOWN], sem=s_t)
            pr = Ring([0, 1, 2, 3])
            pr2 = Ring([4, 5])
            tring = Ring([0, 1])
            qring = Ring([0, 1])
            vring = Ring([0, 1])
            sem_q = [newsem("q"), newsem("q")]
            sem_v = [newsem("v"), newsem("v")]
            groups = [(g * TG, TG) for g in range(TOWN // TG)]
            store_toks = []

            def rope_chunk(wt, wtok, cc, head, is_q):
                for (t0, n) in groups:
                    bi, bank, bfree = pr.get()
                    ps = psum[:, bank, 0:n]
                    last = None
                    for k in range(KC):
                        last = S.op("pe", lambda e, ps=ps, k=k, t0=t0, n=n: e.matmul(ps, lhsT=wt[:, k, cc * 128:(cc + 1) * 128],
                                                                                     rhs=hT[:, k, t0:t0 + n], start=(k == 0), stop=(k == KC - 1)),
                                    deps=[wtok, bfree, t_h] if k == 0 else (), signal=(k == KC - 1))
                    ti, tslot, tfree = tring.get()
                    T = Tb[:, tslot, 0:n]
                    t1 = S.op("act", lambda e: e.activation(out=T, in_=ps, func=AF.Copy), deps=[last, tfree])
                    pr.release(bi, t1)
                    if pending_rope:
                        pending_rope.pop(0)()
                    pending_rope.append(lambda T=T, t1=t1, ti=ti, tslot=tslot, tfree=tfree, t0=t0, n=n, head=head, is_q=is_q:
                                        rope_finish(T, t1, ti, tslot, tfree, t0, n, head, is_q))

            def rope_finish(T, t1, ti, tslot, tfree, t0, n, head, is_q):
                b2i, bank2, b2free = pr2.get()
                ps2 = psum[:, bank2, 0:n]
                t2 = S.op("pe", lambda e: e.matmul(ps2, lhsT=permb, rhs=T, start=True, stop=True), deps=[t1, b2free, t_c])
                tm = tmpf[:, tslot, 0:n]
                t3 = S.op("dve", lambda e: e.tensor_tensor(out=tm, in0=T, in1=cos_s[:, t0:t0 + n], op=ALU.mult), deps=[t1, t_tab, tfree])
                qi, qslot, qfree = qring.get()
                qo = qst[:, qslot, 0:n]
                t4 = S.op("dve", lambda e: e.tensor_tensor(out=qo, in0=ps2, in1=sin_s[:, t0:t0 + n], op=ALU.mult), deps=[t2, qfree])
                pr2.release(b2i, t4)
                t5 = S.op("dve", lambda e: e.tensor_tensor(out=qo, in0=qo, in1=tm, op=ALU.add), deps=[t4, t3])
                tring.release(ti, (t2, t5))
                if is_q:
                    d1 = S.dma("sp", lambda e: e.dma_start(out=q1_scr[head, :, t0:t0 + n], in_=qo[0:64, :]), sem_q[qi], deps=[t5])
                    d2 = S.dma("sp", lambda e: e.dma_start(out=q2_scr[head, :, t0:t0 + n], in_=qo[64:128, :]), sem_q[qi], deps=[t5])
                    qring.release(qi, d2)
                    store_toks.append(d2)
                else:
                    d1 = S.dma("sp", lambda e: e.dma_start(out=k_scr[head, :, tok0 + t0:tok0 + t0 + n], in_=qo), sem_q[qi], deps=[t5])
                    qring.release(qi, d1)
                    store_toks.append(d1)

            pending_rope = []
            for (off, is_q) in ((QOFF, True), (KOFF, False)):
                if is_q and half == 0:
                    continue
                for c0 in range(0, QKW, WNC):
                    wt, wtok, wi = wload(wsrc(w_in, 0, KC, off + c0, WNC), KC, WNC)
                    for cc in range(WNC // 128):
                        rope_chunk(wt, wtok, cc, (c0 // 128) + cc, is_q)
                    wring.release(wi, S.last("pe"))
                    tick()
            while pending_rope:
                pending_rope.pop(0)()
            for c0 in range(0, AW, WNC):
                wt, wtok, wi = wload(wsrc(w_in, 0, KC, VOFF + c0, WNC), KC, WNC)
                for tb in range(NB):
                    bi, bank, bfree = pr.get()
                    ps = psum[:, bank, 0:WNC]
                    last = None
                    for k in range(KC):
                        last = S.op("pe", lambda e, ps=ps, k=k, tb=tb: e.matmul(ps, lhsT=hT[:, k, tb * 128:(tb + 1) * 128], rhs=wt[:, k, :],
                                                                                start=(k == 0), stop=(k == KC - 1)),
                                    deps=[wtok, bfree, t_h] if k == 0 else (), signal=(k == KC - 1))
                    vi, vslot, vfree = vring.get()
                    vo = vst[:, vslot, :]
                    eng = "act" if tb % 2 == 0 else "dve"
                    if eng == "act":
                        t1 = S.op("act", lambda e, vo=vo, ps=ps: e.activation(out=vo, in_=ps, func=AF.Copy), deps=[last, vfree])
                    else:
                        t1 = S.op("dve", lambda e, vo=vo, ps=ps: e.tensor_copy(out=vo, in_=ps), deps=[last, vfree])
                    pr.release(bi, t1)
                    r0 = tok0 + tb * 128
                    d1 = S.dma("sp", lambda e, vo=vo, r0=r0, c0=c0: e.dma_start(out=v_scr[r0:r0 + 128, c0:c0 + WNC], in_=vo), sem_v[vi], deps=[t1])
                    vring.release(vi, d1)
                    store_toks.append(d1)
                wring.release(wi, S.last("pe"))
                tick()
            if half == 0:
                return store_toks
            nch = WNC // 128
            sem_y = newsem("y")
            y_free = None
            cs_free = None
            us_free = None
            for c0 in range(0, CW, WNC):
                wt, wtok, wi = wload(wsrc(w_in, 0, KC, COFF + c0, WNC), KC, WNC)
                tC = []
                for cc in range(nch):
                    for gi, (t0, n) in enumerate(groups + [(-2, 2)]):
                        bi, bank, bfree = pr.get()
                        ps = psum[:, bank, 0:n]
                        last = None
                        for k in range(KC):
                            rhs = hT[:, k, t0:t0 + n] if t0 >= 0 else halo_h[:, k, :]
                            last = S.op("pe", lambda e, ps=ps, k=k, rhs=rhs, cc=cc: e.matmul(ps, lhsT=wt[:, k, cc * 128:(cc + 1) * 128], rhs=rhs,
                                                                                            start=(k == 0), stop=(k == KC - 1)),
                                        deps=[wtok, bfree, t_h] if k == 0 else (), signal=(k == KC - 1))
                        dst = csb[:, cc, 2 + t0:2 + t0 + n]
                        t1 = S.op("act", lambda e, dst=dst, ps=ps: e.activation(out=dst, in_=ps, func=AF.Copy), deps=[last, cs_free])
                        pr.release(bi, t1)
                        tC.append(t1)
                wring.release(wi, S.last("pe"))
                tick()
                wt, wtok, wi = wload(wsrc(w_in, 0, KC, XOFF + c0, WNC), KC, WNC)
                tU = []
                for cc in range(nch):
                    for gi, (t0, n) in enumerate(groups + [(-2, 2)]):
                        bi, bank, bfree = pr.get()
                        ps = psum[:, bank, 0:n]
                        last = None
                        for k in range(KC):
                            rhs = hT[:, k, t0:t0 + n] if t0 >= 0 else halo_h[:, k, :]
                            last = S.op("pe", lambda e, ps=ps, k=k, rhs=rhs, cc=cc: e.matmul(ps, lhsT=wt[:, k, cc * 128:(cc + 1) * 128], rhs=rhs,
                                                                                            start=(k == 0), stop=(k == KC - 1)),
                                        deps=[wtok, bfree, t_h] if k == 0 else (), signal=(k == KC - 1))
                        dst = usb[:, cc, 2 + t0:2 + t0 + n]
                        src = csb[:, cc, 2 + t0:2 + t0 + n]
                        t1 = S.op("dve", lambda e, dst=dst, ps=ps, src=src: e.tensor_tensor(out=dst, in0=ps, in1=src, op=ALU.mult),
                                  deps=[last, us_free] + tC)
                        pr.release(bi, t1)
                        if t0 < 0:
                            t1 = S.op("dve", lambda e, dst=dst: e.tensor_scalar(out=dst, in0=dst, scalar1=flags_s[:, 1:2], scalar2=None, op0=ALU.mult),
                                      deps=[t1, t_c])
                        tU.append(t1)
                wring.release(wi, S.last("pe"))
                tick()
                cs_free = tU[-1]
                tY = []
                for cc in range(nch):
                    ch = c0 // 128 + cc
                    w0 = convw_s[:, ch * 3 + 0:ch * 3 + 1]
                    w1 = convw_s[:, ch * 3 + 1:ch * 3 + 2]
                    w2 = convw_s[:, ch * 3 + 2:ch * 3 + 3]
                    yv = ycv[:, cc, :]
                    t1 = S.op("dve", lambda e, yv=yv, cc=cc, w2=w2: e.tensor_scalar(out=yv, in0=usb[:, cc, 2:2 + TOWN], scalar1=w2, scalar2=None, op0=ALU.mult),
                              deps=tU + [y_free, t_c])
                    t2 = S.op("dve", lambda e, yv=yv, cc=cc, w1=w1: e.scalar_tensor_tensor(out=yv, in0=usb[:, cc, 1:1 + TOWN], scalar=w1, in1=yv,
                                                                                          op0=ALU.mult, op1=ALU.add), deps=[t1])
                    t3 = S.op("dve", lambda e, yv=yv, cc=cc, w0=w0: e.scalar_tensor_tensor(out=yv, in0=usb[:, cc, 0:TOWN], scalar=w0, in1=yv,
                                                                                          op0=ALU.mult, op1=ALU.add), deps=[t2])
                    tY.append(t3)
                us_free = tY[-1]
                wt, wtok, wi = wload(wsrc(w_in, 0, KC, BOFF + c0, WNC), KC, WNC)
                tS = []
                for cc in range(nch):
                    for (t0, n) in groups:
                        bi, bank, bfree = pr.get()
                        ps = psum[:, bank, 0:n]
                        last = None
                        for k in range(KC):
                            last = S.op("pe", lambda e, ps=ps, k=k, t0=t0, n=n, cc=cc: e.matmul(ps, lhsT=wt[:, k, cc * 128:(cc + 1) * 128],
                                                                                               rhs=hT[:, k, t0:t0 + n], start=(k == 0), stop=(k == KC - 1)),
                                        deps=[wtok, bfree, t_h] if k == 0 else (), signal=(k == KC - 1))
                        dst = yst[:, cc, t0:t0 + n]
                        src = ycv[:, cc, t0:t0 + n]
                        t1 = S.op("dve", lambda e, dst=dst, ps=ps, src=src: e.tensor_tensor(out=dst, in0=ps, in1=src, op=ALU.mult),
                                  deps=[last, y_free] + tY)
                        pr.release(bi, t1)
                        tS.append(t1)
                wring.release(wi, S.last("pe"))
                tick()
                d = None
                for cc in range(nch):
                    ch = c0 // 128 + cc
                    d = S.dma("sp", lambda e, cc=cc, ch=ch: e.dma_start(out=y_scr[ch, :, :], in_=yst[:, cc, :]), sem_y, deps=tS)
                y_free = d
                store_toks.append(d)
            return store_toks

        all_stores = []
        for half in (0, 1):
            src = xin[TOWN:TALL, :] if half == 0 else xin[0:TOWN, :]
            t_h = norm_transpose(car.off if False else off3, src, NB, hT, gm1, modT[:, 0:KC], t_dst_free=(S.last("pe")))
            if half == 0:
                t_halo = S.op("dve", lambda e: e.tensor_copy(out=halo_h, in_=hT[:, :, TOWN - 2:TOWN]), deps=[t_h])
                t_h = (t_h, t_halo)
            if STOP == 20 + half:
                S.barrier()
                raise _StopBuild()
            all_stores += in_proj(half)
            if STOP == 30 + half:
                S.barrier()
                raise _StopBuild()
            S.barrier()
        while ada_next[0] < 3 * GPM:
            ada_more(1)
        s_m = newsem("m")
        mod_scr_T = mod_scr.rearrange("(j p) -> p j", p=128)
        t_modscr1 = S.dma("sp", lambda e: e.dma_start(out=mod_scr_T[:, 2 * KC:3 * KC], in_=modT[:, 2 * KC:3 * KC]), s_m, deps=[ada_done[0]])
        S.barrier()

        if STOP == 3:
            S.barrier()
            raise _StopBuild()
        car.off = 0
        catT = car.bf([KC, TOWN])
        off4 = car.off
        qq = car.bf([2, 2, TOWN])
        kh = car.bf([2, TALL])
        vraw = car.bf([2, TALL // 128, 128])
        vh = car.bf([2, TALL // 128, 129])
        ptl = car.bf([3, 512])
        o1n = car.f32([2, 128])
        osb = car.f32([2, 128])
        onb = car.bf([2, 128])
        junk4 = car.f32([128])
        s_y2 = newsem("yl")

        def load_y_chunks():
            for ch in range(CC):
                ld(catT[:, H + ch, :], y_scr[ch, :, :], sem=s_y2, deps=all_stores[-1:])
        t = S.op("dve", lambda e: e.memset(qq[64:128, :, 0, :], 0.0))
        t = S.op("dve", lambda e: e.memset(qq[0:64, :, 1, :], 0.0))
        t_init = S.op("dve", lambda e: e.memset(vh[:, :, :, 128:129], 1.0))
        sem_h = [newsem("h"), newsem("h")]
        hring = Ring([0, 1])
        sring = Ring([0, 1, 2, 3])
        oring = Ring([4, 5])
        pring = Ring([0, 1, 2])
        ering = Ring([0, 1])
        NOB = NB
        st = small[:, 64:64 + 32 * 8]
        eidx = [0]
        tring4 = Ring([6])
        tasks = []
        for h in range(H):
            for i in range(NB):
                kb_oth = [(NB + j, True, False) for j in range(NOB)]
                kb_own = [(j, False, j == i) for j in range(i + 1)]
                grps = [kb_oth[a:a + 2] for a in range(0, len(kb_oth), 2)] + [kb_own[a:a + 2] for a in range(0, len(kb_own), 2)]
                for gi_, grp in enumerate(grps):
                    tasks.append(dict(h=h, i=i, grp=grp, first=(gi_ == 0), last=(gi_ == len(grps) - 1)))
        head_state = {}
        qb_state = {}
        pending_T = []

        def head_prologue(h):
            hi, hs, hfree = hring.get()
            tq = S.dma("sp", lambda e: e.dma_start(out=qq[0:64, hs, 0, :], in_=q1_scr[h, :, :]), sem_h[hi], deps=[hfree] + all_stores)
            tq = S.dma("sp", lambda e: e.dma_start(out=qq[64:128, hs, 1, :], in_=q2_scr[h, :, :]), sem_h[hi])
            tq = S.dma("sp", lambda e: e.dma_start(out=kh[:, hs, :], in_=k_scr[h, :, :]), sem_h[hi])
            tq = S.dma("sp", lambda e: e.dma_start(out=vraw[:, hs, :, :],
                                                   in_=v_scr[:, h * 128:(h + 1) * 128].rearrange("(b p) d -> p b d", p=128)), sem_h[hi])
            tv = S.op("dve", lambda e: e.tensor_copy(out=vh[:, hs, :, 0:128], in_=vraw[:, hs, :, :]), deps=[tq, t_init, hfree])
            head_state[h] = dict(hi=hi, hs=hs, tv=tv)

        def phase_A(tk):
            h, i, grp = tk["h"], tk["i"], tk["grp"]
            if h not in head_state:
                head_prologue(h)
            hs, tv = head_state[h]["hs"], head_state[h]["tv"]
            ng = len(grp)
            si, sbank, sfree = sring.get()
            lastS = None
            for gi, (kb, oth, diag) in enumerate(grp):
                ps = psum[:, sbank, gi * 256:gi * 256 + 256]
                lastS = S.op("pe", lambda e: e.matmul(ps, lhsT=kh[:, hs, kb * 128:(kb + 1) * 128], rhs=qq[:, hs, :, i * 128:(i + 1) * 128],
                                                      start=True, stop=not diag),
                             deps=[tv, sfree, t_init], signal=not diag)
                if diag:
                    lastS = S.op("pe", lambda e: e.matmul(ps, lhsT=identb, rhs=trimask2, start=False, stop=True), deps=[t_c])
            pi, pslot, pfree = pring.get()
            bias = flags_s[:, 0:1] if grp[0][1] else 0.0
            nn = ng * 256
            pa = ptl[:, pslot, 0:nn]
            sview = psum[:, sbank, 0:nn]
            e1 = S.op("act", lambda e: e.activation(out=pa, in_=sview, func=AF.Exp, scale=0.125, bias=bias), deps=[lastS, pfree, t_c])
            e2 = e1
            sring.release(si, e2)
            tk.update(pi=pi, pslot=pslot, e1=e1, e2=e2)

        def phase_B(tk):
            h, i, grp = tk["h"], tk["i"], tk["grp"]
            hs, tv = head_state[h]["hs"], head_state[h]["tv"]
            if tk["first"]:
                oi, obank, ofree = oring.get()
                qb_state[(h, i)] = dict(oi=oi, obank=obank, ofree=ofree)
            qs = qb_state[(h, i)]
            O = psum[:, qs["obank"], 0:258].rearrange("p (a b) -> p a b", a=2)
            pslot = tk["pslot"]
            lastPV = None
            for gi, (kb, oth, diag) in enumerate(grp):
                pt, ee = ptl, tk["e1"]
                for a in (0, 1):
                    st_ = tk["first"] and gi == 0 and a == 0
                    fin = tk["last"] and gi == len(grp) - 1
                    c0_ = gi * 256 + a * 128
                    lastPV = S.op("pe", lambda e: e.matmul(O[:, a, :], lhsT=pt[:, pslot, c0_:c0_ + 128], rhs=vh[:, hs, kb, :],
                                                           start=st_, stop=fin, skip_group_check=True),
                                  deps=[ee, qs["ofree"], tv], signal=(gi == len(grp) - 1 and a == 1))
            pring.release(tk["pi"], lastPV)
            head_state[h]["last_pe"] = lastPV
            if not tk["last"]:
                return
            ei, es_, efree = ering.get()
            k0 = (eidx[0] % 32) * 8
            eidx[0] += 1
            sc = st[:, k0:k0 + 8]
            t1 = S.op("dve", lambda e: e.reciprocal(out=sc[:, 0:2], in_=O[:, :, 128]), deps=[lastPV, efree])
            t2 = S.op("dve", lambda e: e.tensor_tensor(out=sc[:, 2:3], in0=sc[:, 1:2], in1=lamw[:, 4:5], op=ALU.mult), deps=[t1, t_consts])
            o1 = o1n[:, es_, :]
            t3 = S.op("dve", lambda e: e.tensor_scalar(out=o1, in0=O[:, 0, 0:128], scalar1=sc[:, 0:1], scalar2=None, op0=ALU.mult), deps=[t1])
            ob = osb[:, es_, :]
            t4 = S.op("dve", lambda e: e.scalar_tensor_tensor(out=ob, in0=O[:, 1, 0:128], scalar=sc[:, 2:3], in1=o1,
                                                              op0=ALU.mult, op1=ALU.add), deps=[t2, t3])
            oring.release(qs["oi"], t4)
            t5 = S.op("act", lambda e: e.activation(out=junk4, in_=ob, func=AF.Square, accum_out=sc[:, 3:4]), deps=[t4])
            t6 = S.op("act", lambda e: e.activation(out=sc[:, 4:5], in_=sc[:, 3:4], func=AF.Ln, scale=1.0 / 128, bias=epsb), deps=[t5])
            t7 = S.op("act", lambda e: e.activation(out=sc[:, 5:6], in_=sc[:, 4:5], func=AF.Exp, scale=-0.5), deps=[t6])
            on = onb[:, es_, :]
            t8 = S.op("dve", lambda e: e.scalar_tensor_tensor(out=on, in0=ob, scalar=sc[:, 5:6], in1=gsub,
                                                              op0=ALU.mult, op1=ALU.mult), deps=[t7])
            pending_T.append(dict(h=h, i=i, on=on, t8=t8, ei=ei, age=0))
            if tk["i"] == NB - 1:
                hring.release(head_state[h]["hi"], lastPV)

        def flush_T(min_age):
            keep = []
            for pt_ in pending_T:
                if pt_["age"] < min_age:
                    pt_["age"] += 1
                    keep.append(pt_)
                    continue
                ti, tbank, tfree = tring4.get()
                tps = psum[:, tbank, 0:64].bitcast(BF16)
                on, h, i = pt_["on"], pt_["h"], pt_["i"]
                t9 = S.op("pe", lambda e: e.transpose(out=tps, in_=on, identity=identb), deps=[pt_["t8"], tfree])
                t10 = S.op("dve", lambda e: e.tensor_copy(out=catT[:, h, i * 128:(i + 1) * 128], in_=tps), deps=[t9])
                tring4.release(ti, t10)
                ering.release(pt_["ei"], (t9, pt_["t8"]))
            pending_T[:] = keep

        n_t = len(tasks)
        ADA_ATT = NGRP
        ada_every = max(1, n_t // max(1, n_ktiles * (ADA_ATT - ada_next[0])))
        for t_i in range(n_t + 2):
            if t_i == min(6, n_t):
                load_y_chunks()
            if t_i < n_t:
                phase_A(tasks[t_i])
            if t_i >= 2:
                phase_B(tasks[t_i - 2])
                flush_T(3)
            if t_i % ada_every == ada_every - 1:
                ada_step(ADA_ATT)
        flush_T(0)
        flush_T(0)
        while ada_next[0] < ADA_ATT:
            ada_more(1)
        S.op("dve", lambda e: e.scalar_tensor_tensor(out=gm2, in0=modT[:, 4 * KC:5 * KC], scalar=1.0, in1=n2g_s,
                                                     op0=ALU.add, op1=ALU.mult), deps=[ada_done[0], t_c])
        t_ycat = (s_y2, S.dcnt[s_y2])
        t_cat = (S.last("act"), S.last("dve"), t_ycat)
        S.barrier()

        if STOP == 4:
            S.barrier()
            raise _StopBuild()
        car.off = off4
        g1bc = car.f32([D])
        xcols = car.f32([2, NB, WNC])
        x1st = car.f32([2, NB, WNC])
        s_g = newsem("g")
        t_g1 = ld(g1bc, mod_scr[2 * D:3 * D].partition_broadcast(128), sem=s_g, deps=[t_modscr1])
        xr = Ring([0, 1])
        sem_xc = [newsem("xc"), newsem("xc")]
        sem_x1 = [newsem("x1"), newsem("x1")]
        pr = Ring([0, 1, 2, 3])
        x1_stores = []
        for c0 in range(0, D, WNC):
            wt, wtok, wi = wload(wsrc(w_out, 0, KC, c0, WNC), KC, WNC)
            xi, xslot, xfree = xr.get()
            tx = S.dma("sp", lambda e, xslot=xslot, c0=c0: e.dma_start(out=xcols[:, xslot, :, :],
                                                                      in_=xin[0:TOWN, c0:c0 + WNC].rearrange("(b p) n -> p b n", p=128)),
                       sem_xc[xi], deps=[xfree])
            tl = []
            for tb in range(NB):
                bi, bank, bfree = pr.get()
                ps = psum[:, bank, 0:WNC]
                last = None
                for k in range(KC):
                    last = S.op("pe", lambda e, ps=ps, k=k, tb=tb: e.matmul(ps, lhsT=catT[:, k, tb * 128:(tb + 1) * 128], rhs=wt[:, k, :],
                                                                            start=(k == 0), stop=(k == KC - 1)),
                                deps=[wtok, bfree, t_cat] if k == 0 else (), signal=(k == KC - 1))
                dst = x1st[:, xslot, tb, :]
                t1 = S.op("dve", lambda e, dst=dst, ps=ps, c0=c0: e.tensor_tensor(out=dst, in0=ps, in1=g1bc[:, c0:c0 + WNC], op=ALU.mult),
                          deps=[last, t_g1, xfree])
                pr.release(bi, t1)
                t2 = S.op("dve", lambda e, dst=dst, xslot=xslot, tb=tb: e.tensor_tensor(out=dst, in0=dst, in1=xcols[:, xslot, tb, :], op=ALU.add),
                          deps=[t1, tx])
                tl.append(t2)
            wring.release(wi, S.last("pe"))
            d = S.dma("sp", lambda e, xslot=xslot, c0=c0: e.dma_start(out=x1_scr[:, c0:c0 + WNC].rearrange("(b p) n -> p b n", p=128),
                                                                     in_=x1st[:, xslot, :, :]), sem_x1[xi], deps=tl)
            xr.release(xi, d)
            x1_stores.append(d)
            if (c0 // WNC) % 2 == 1:
                ada_more(1)
        while ada_next[0] < NGRP:
            ada_more(1)
        t_modscr = S.dma("sp", lambda e: e.dma_start(out=mod_scr_T[:, 5 * KC:6 * KC], in_=modT[:, 5 * KC:6 * KC]), s_m, deps=[ada_done[0]])
        S.barrier()

        if STOP == 5:
            S.barrier()
            raise _StopBuild()
        car.off = 0
        h2T = car.bf([KC, PASS])
        x2 = car.f32([PB, D])
        g2bc = car.f32([D])
        off7 = car.off
        nparts = 8 if FC >= 8 else 2
        base, rem = divmod(FC, nparts)
        parts = [base + (1 if i < rem else 0) for i in range(nparts)]
        PMAX = max(parts)
        DNC = 512 if (PMAX * 512 <= cfg.WCAP and D % 512 == 0) else WNC
        s_fg = newsem("fg")
        sem_x2 = [newsem("x2") for _ in range(PB)]
        sem_o = newsem("o")
        out_toks = []
        t_prev_pass = None
        for p in range(NPASS):
            r0 = p * PASS
            c = Carver()
            c.off = off7
            junk7 = c.bf([D])
            xn6 = c.f32([D])
            xn6_free = [None]
            t_g2 = ld(g2bc, mod_scr[5 * D:6 * D].partition_broadcast(128), sem=s_g, deps=[t_modscr, t_prev_pass])
            tx2 = []
            for tb in range(PB):
                tx2.append(S.dma("sp", lambda e: e.dma_start(out=x2[:, tb, :], in_=x1_scr[r0 + tb * 128:r0 + (tb + 1) * 128, :]),
                                 sem_x2[tb], deps=x1_stores + [t_prev_pass]))
            pr6 = Ring([0, 1, 2, 3, 4, 5])

            def stats6(tb):
                k0 = (tb % 8) * 4
                xv = x2[:, tb, :]
                t1 = S.op("act", lambda e: e.activation(out=junk7, in_=xv, func=AF.Square, accum_out=small[:, k0:k0 + 1]),
                          deps=[tx2[tb], t_prev_pass])
                t2 = S.op("act", lambda e: e.activation(out=small[:, k0 + 1:k0 + 2], in_=small[:, k0:k0 + 1], func=AF.Sqrt,
                                                        scale=1.0 / D, bias=epsb), deps=[t1])
                t3 = S.op("dve", lambda e: e.reciprocal(out=small[:, k0 + 2:k0 + 3], in_=small[:, k0 + 1:k0 + 2]), deps=[t2])
                t4 = S.op("act", lambda e: e.activation(out=xn6, in_=xv, func=AF.Copy, scale=small[:, k0 + 2:k0 + 3]),
                          deps=[t3, xn6_free[0]])
                return dict(t4=t4)

            def mm_evac6(tb, sb):
                for k4 in range(0, KC, 4):
                    bi, bank, bfree = pr6.get()
                    tp = None
                    for q in range(4):
                        k = k4 + q
                        tp = S.op("pe", lambda e: e.transpose(out=psum[:, bank, q * 128:(q + 1) * 128], in_=xn6[:, k * 128:(k + 1) * 128], identity=identf),
                                  deps=[sb["t4"], bfree, t_prev_pass, t_c] if q == 0 else (), signal=(q == 3))
                    eng = "act" if ((k4 // 4) % 8) == 0 else "dve"
                    lastev = None
                    for q in range(4):
                        k = k4 + q
                        dst = h2T[:, k, tb * 128:(tb + 1) * 128]
                        src = psum[:, bank, q * 128:(q + 1) * 128]
                        if eng == "act":
                            lastev = S.op("act", lambda e: e.activation(out=dst, in_=src, func=AF.Identity, scale=gm2[:, k:k + 1],
                                                                        bias=modT[:, 3 * KC + k:3 * KC + k + 1]), deps=[tp])
                        else:
                            lastev = S.op("dve", lambda e: e.tensor_scalar(out=dst, in0=src, scalar1=gm2[:, k:k + 1],
                                                                           scalar2=modT[:, 3 * KC + k:3 * KC + k + 1], op0=ALU.mult, op1=ALU.add), deps=[tp])
                    pr6.release(bi, lastev)
                xn6_free[0] = S.last("pe")

            for tb in range(PB):
                mm_evac6(tb, stats6(tb))
            t_h = (S.last("act"), S.last("dve"))
            S.barrier()
            c = Carver()
            c.off = off7
            gT = c.bf([PMAX, PASS])
            sg = c.f32([2, PASS])
            dtmp = c.f32([2, DNC])
            gring = Ring([0, 1, 2, 3])
            dring = Ring([4, 5])
            sgr = Ring([0, 1])
            dtr = Ring([0, 1])
            f0 = 0
            t_down_last = None
            t_x2_last = None
            for pi_, nf in enumerate(parts):
                tg_l = []
                fl = 0
                while fl < nf:
                    ncol = min(WNC, (nf - fl) * 128)
                    ncc = ncol // 128
                    col0 = (f0 + fl) * 128
                    wg, wgtok, wgi = wload(wsrc(w_gate, 0, KC, col0, ncol), KC, ncol)
                    sgs = []
                    for cc in range(ncc):
                        bi, bank, bfree = gring.get()
                        ps = psum[:, bank, 0:PASS]
                        last = None
                        for k in range(KC):
                            last = S.op("pe", lambda e: e.matmul(ps, lhsT=wg[:, k, cc * 128:(cc + 1) * 128], rhs=h2T[:, k, :],
                                                                 start=(k == 0), stop=(k == KC - 1)),
                                        deps=[wgtok, bfree, t_h, t_down_last] if k == 0 else (), signal=(k == KC - 1))
                        si, ss_, sfree = sgr.get()
                        sgt = sg[:, ss_, :]
                        t1 = S.op("act", lambda e: e.activation(out=sgt, in_=ps, func=AF.Silu), deps=[last, sfree])
                        gring.release(bi, t1)
                        sgs.append((si, sgt, t1))
                    wring.release(wgi, S.last("pe"))
                    wu, wutok, wui = wload(wsrc(w_up, 0, KC, col0, ncol), KC, ncol)
                    for cc in range(ncc):
                        bi, bank, bfree = gring.get()
                        ps = psum[:, bank, 0:PASS]
                        last = None
                        for k in range(KC):
                            last = S.op("pe", lambda e: e.matmul(ps, lhsT=wu[:, k, cc * 128:(cc + 1) * 128], rhs=h2T[:, k, :],
                                                                 start=(k == 0), stop=(k == KC - 1)),
                                        deps=[wutok, bfree] if k == 0 else (), signal=(k == KC - 1))
                        si, sgt, ts = sgs[cc]
                        dst = gT[:, fl + cc, :]
                        t2 = S.op("dve", lambda e: e.tensor_tensor(out=dst, in0=ps, in1=sgt, op=ALU.mult), deps=[last, ts, t_down_last])
                        gring.release(bi, t2)
                        sgr.release(si, t2)
                        tg_l.append(t2)
                    wring.release(wui, S.last("pe"))
                    fl += ncc
                for c0 in range(0, D, DNC):
                    wd, wdtok, wdi = wload(wsrc(w_down, f0 * 128, nf, c0, DNC), nf, DNC)
                    for tb in range(PB):
                        bi, bank, bfree = dring.get()
                        ps = psum[:, bank, 0:DNC]
                        last = None
                        for k in range(nf):
                            last = S.op("pe", lambda e: e.matmul(ps, lhsT=gT[:, k, tb * 128:(tb + 1) * 128], rhs=wd[:, k, :],
                                                                 start=(k == 0), stop=(k == nf - 1)),
                                        deps=[wdtok, bfree] + tg_l if k == 0 else (), signal=(k == nf - 1))
                        di, ds, dfree = dtr.get()
                        dt_ = dtmp[:, ds, :]
                        t1 = S.op("dve", lambda e: e.tensor_tensor(out=dt_, in0=ps, in1=g2bc[:, c0:c0 + DNC], op=ALU.mult),
                                  deps=[last, dfree, t_g2])
                        dring.release(bi, t1)
                        xv = x2[:, tb, c0:c0 + DNC]
                        t2 = S.op("dve", lambda e: e.tensor_tensor(out=xv, in0=xv, in1=dt_, op=ALU.add), deps=[t1, tx2[tb]])
                        dtr.release(di, t2)
                        t_x2_last = t2
                    wring.release(wdi, S.last("pe"))
                t_down_last = S.last("pe")
                f0 += nf
            tfg = ld(g2bc, fgv.partition_broadcast(128), sem=s_fg, deps=[t_x2_last])
            d = None
            for tb in range(PB):
                k0 = (tb % 8) * 4
                xv = x2[:, tb, :]
                t1 = S.op("act", lambda e: e.activation(out=h2T.rearrange("p a b -> p (a b)")[:, 0:D], in_=xv, func=AF.Square,
                                                        accum_out=small[:, k0:k0 + 1]), deps=[t_x2_last, t_down_last])
                t2 = S.op("act", lambda e: e.activation(out=small[:, k0 + 1:k0 + 2], in_=small[:, k0:k0 + 1], func=AF.Sqrt,
                                                        scale=1.0 / D, bias=epsb), deps=[t1])
                t3 = S.op("dve", lambda e: e.reciprocal(out=small[:, k0 + 2:k0 + 3], in_=small[:, k0 + 1:k0 + 2]), deps=[t2])
                t4 = S.op("dve", lambda e: e.scalar_tensor_tensor(out=xv, in0=xv, scalar=small[:, k0 + 2:k0 + 3], in1=g2bc,
                                                                  op0=ALU.mult, op1=ALU.mult), deps=[t3, tfg])
                d = S.dma("sp", lambda e: e.dma_start(out=out[r0 + tb * 128:r0 + (tb + 1) * 128, :], in_=xv), sem_o, deps=[t4])
                out_toks.append(d)
            t_prev_pass = (d, S.last("dve"), S.last("act"), S.last("pe"))
            S.barrier()

    except _StopBuild:
        pass
    S.ops["sp"].append((S._waits("sp", out_toks), None, None, 0, 0))

    with nc.allow_non_contiguous_dma(reason="tiny transposed store of modulation vectors"):
        with nc.Block() as block:
            S.emit(block, sems)
    es.close()
    return nc


def host_consts(cfg, half):
    bf = ml_dtypes.bfloat16
    TOWN, TALL = cfg.TOWN, cfg.TALL
    own0 = half * TOWN
    oth0 = (1 - half) * TOWN
    pos = np.concatenate([np.arange(own0, own0 + TOWN), np.arange(oth0, oth0 + TOWN)]).astype(np.float32)
    inv = (ROPE_THETA ** (-np.arange(32, dtype=np.float32) / 32)).astype(np.float32)
    ang = pos[None, :] * inv[:, None]
    cos = np.cos(ang).astype(np.float32)
    sin = np.sin(ang).astype(np.float32)
    cosT = np.tile(cos, (4, 1))
    sinsT = np.concatenate([-sin, sin, -sin, sin], axis=0)
    ident = np.eye(128, dtype=np.float32)
    perm = np.zeros((128, 128), np.float32)
    for m in range(128):
        perm[m ^ 32, m] = 1.0
    p = np.arange(128)[:, None]
    cidx = np.arange(128)[None, :]
    tri = np.where(p <= cidx, 0.0, NEG).astype(np.float32)
    flags = np.zeros((128, 2), np.float32)
    flags[:, 0] = 0.0 if half == 1 else NEG
    flags[:, 1] = 1.0 if half == 1 else 0.0
    return dict(cosT=np.ascontiguousarray(cosT), sinsT=np.ascontiguousarray(sinsT), identf=ident, identb=ident.astype(bf),
                permb=perm.astype(bf), trimask=np.concatenate([tri, tri], axis=1).astype(bf), flags=flags)


def make_in_maps(cfg, x, c, w_ada, b_ada, norm1_g, w_in, lambda_q1, lambda_k1, lambda_q2, lambda_k2, subln_g, conv_w, w_out,
                 norm2_g, w_gate, w_up, w_down, final_g, n_cores):
    f = np.float32
    D, KC, TOWN = cfg.D, cfg.KC, cfg.TOWN

    def colT(v, nchunk):
        return np.ascontiguousarray(np.asarray(v, f).reshape(nchunk, 128).T)

    shared = dict(
        w_ada=np.ascontiguousarray(np.asarray(w_ada[0], f)), badaT=colT(b_ada[0], 6 * KC),
        n1gT=colT(norm1_g[0], KC), n2gT=colT(norm2_g[0], KC), fgv=np.ascontiguousarray(np.asarray(final_g, f)),
        w_in=np.ascontiguousarray(np.asarray(w_in[0], f)), w_out=np.ascontiguousarray(np.asarray(w_out[0], f)),
        w_gate=np.ascontiguousarray(np.asarray(w_gate[0], f)), w_up=np.ascontiguousarray(np.asarray(w_up[0], f)),
        w_down=np.ascontiguousarray(np.asarray(w_down[0], f)),
        lamv=np.concatenate([np.asarray(v[0], f) for v in (lambda_q1, lambda_k1, lambda_q2, lambda_k2)]),
        sublng=np.ascontiguousarray(np.asarray(subln_g[0], f)),
        convwT=np.ascontiguousarray(np.asarray(conv_w[0], f).reshape(3, cfg.CC, 128).transpose(2, 1, 0).reshape(128, cfg.CC * 3)),
    )
    hc = [host_consts(cfg, 0), host_consts(cfg, 1)]
    maps = []
    x = np.asarray(x, f)
    c = np.asarray(c, f)
    for core in range(n_cores):
        b, half = core // 2, core % 2
        xb = x[b]
        xin = np.concatenate([xb[half * TOWN:(half + 1) * TOWN], xb[(1 - half) * TOWN:(2 - half) * TOWN]], axis=0)
        m = dict(shared)
        m.update(hc[half])
        m["xin"] = np.ascontiguousarray(xin)
        m["cT"] = colT(c[b], KC)
        maps.append(m)
    return maps


_NC_CACHE = {}


def kernel(**inputs):
    cfg = Cfg()
    if "nc" not in _NC_CACHE:
        _NC_CACHE["nc"] = build(cfg)
    nc = _NC_CACHE["nc"]
    maps = make_in_maps(cfg, n_cores=8, **inputs)
    res = run_bass_kernel_spmd(nc, maps, core_ids=list(range(8)))
    B = inputs["x"].shape[0]
    outp = np.empty((B, cfg.S, cfg.D), np.float32)
    for core in range(8):
        b, half = core // 2, core % 2
        outp[b, half * cfg.TOWN:(half + 1) * cfg.TOWN] = res.results[core]["out"]
    return outp
```
